# Optimizing a Trainium2 kernel written in Bass

```python
import math
import jax, jax.numpy as jnp
from jax import lax
import numpy as np

D_MODEL = 4096
BATCH = 2
SEQ = 8192
DEPTH = 1

HG_HEADS = 16
HG_DK = 128
HG_DV = 128
HG_KW = HG_HEADS * HG_DK
HG_VW = HG_HEADS * HG_DV
HG_CHUNK = 64

MLA_HEADS = 16
Q_RANK = 768
KV_RANK = 512
NOPE_D = 128
ROPE_D = 64
QK_D = NOPE_D + ROPE_D
V_D = 128
MLA_VW = MLA_HEADS * V_D
ROPE_THETA = 10000.0
Q_BLOCK = 128

MIX_W = HG_VW + MLA_VW
D_FF = 4 * D_MODEL
EPS = 1e-6

IN_SPLITS = [HG_KW, HG_KW, HG_VW, HG_VW, Q_RANK, KV_RANK, ROPE_D]
IN_W = sum(IN_SPLITS)

kernel_name = "hymba_hgrn2_mla_sqrelu_block"


def rms_norm(x, w):
    xf = x.astype(jnp.float32)
    y = xf * lax.rsqrt(jnp.mean(xf * xf, axis=-1, keepdims=True) + EPS)
    return (y * w.astype(jnp.float32)).astype(x.dtype)


def rope(x, pos):
    d = x.shape[-1]
    inv_freq = ROPE_THETA ** (-jnp.arange(0, d, 2, dtype=jnp.float32) / d)
    ang = pos.astype(jnp.float32)[..., None] * inv_freq
    cos = jnp.cos(ang)[:, :, None, :]
    sin = jnp.sin(ang)[:, :, None, :]
    xf = x.astype(jnp.float32)
    x1, x2 = xf[..., : d // 2], xf[..., d // 2:]
    out = jnp.concatenate([x1 * cos - x2 * sin, x2 * cos + x1 * sin], axis=-1)
    return out.astype(x.dtype)


def hgrn2_mixer(q_raw, f_raw, i_raw, g_raw, lb, out_norm_w):
    B, S, _ = q_raw.shape
    dt = q_raw.dtype
    N = S // HG_CHUNK
    lbf = lb.astype(jnp.float32)
    q = jax.nn.silu(q_raw.astype(jnp.float32))
    f = lbf + (1.0 - lbf) * jax.nn.sigmoid(f_raw.astype(jnp.float32))
    k = 1.0 - f
    logf = jnp.log(f)
    v = i_raw.astype(jnp.float32)

    def to_chunks(t, d):
        return t.reshape(B, N, HG_CHUNK, HG_HEADS, d).transpose(0, 3, 1, 2, 4)

    q, k, logf = (to_chunks(t, HG_DK) for t in (q, k, logf))
    v = to_chunks(v, HG_DV)
    b = jnp.cumsum(logf, axis=3)
    b_last = b[:, :, :, -1:, :]
    q_dec = q * jnp.exp(b)
    k_inv = k * jnp.exp(-b)
    k_tail = k * jnp.exp(b_last - b)
    chunk_decay = jnp.exp(b_last[:, :, :, 0, :])

    a = jnp.einsum('bhncd,bhnsd->bhncs', q_dec, k_inv)
    causal = jnp.tril(jnp.ones((HG_CHUNK, HG_CHUNK), dtype=bool))
    a = jnp.where(causal, a, 0.0)
    o_intra = jnp.einsum('bhncs,bhnsv->bhncv', a, v)

    def step(state, xs):
        qd, kt, vv, dec = xs
        o = jnp.einsum('bhcd,bhdv->bhcv', qd, state)
        state = dec[..., None] * state + jnp.einsum('bhcd,bhcv->bhdv', kt, vv)
        return state, o

    s0 = jnp.zeros((B, HG_HEADS, HG_DK, HG_DV), jnp.float32)
    xs = (jnp.moveaxis(q_dec, 2, 0), jnp.moveaxis(k_tail, 2, 0),
          jnp.moveaxis(v, 2, 0), jnp.moveaxis(chunk_decay, 2, 0))
    _, o_inter = lax.scan(step, s0, xs)
    o = o_intra + jnp.moveaxis(o_inter, 0, 2)
    o = o.transpose(0, 2, 3, 1, 4).reshape(B, S, HG_HEADS, HG_DV)

    o = rms_norm(o, out_norm_w)
    gate = jax.nn.silu(g_raw.astype(jnp.float32)).reshape(B, S, HG_HEADS, HG_DV)
    return (o.astype(jnp.float32) * gate).reshape(B, S, HG_VW).astype(dt)


def mla_mixer(c_q, c_kv, k_rope, positions, q_norm_w, w_uq, kv_norm_w, w_ukv,
              q_head_norm_w, k_head_norm_w):
    B, S, _ = c_q.shape
    dt = c_q.dtype
    q = (rms_norm(c_q, q_norm_w) @ w_uq).reshape(B, S, MLA_HEADS, QK_D)
    kv = (rms_norm(c_kv, kv_norm_w) @ w_ukv).reshape(B, S, MLA_HEADS, NOPE_D + V_D)
    k_nope, v = kv[..., :NOPE_D], kv[..., NOPE_D:]
    k_pe = jnp.broadcast_to(k_rope[:, :, None, :], (B, S, MLA_HEADS, ROPE_D))
    k = jnp.concatenate([k_nope, k_pe], axis=-1)

    q = rms_norm(q, q_head_norm_w)
    k = rms_norm(k, k_head_norm_w)
    q = jnp.concatenate([q[..., :NOPE_D], rope(q[..., NOPE_D:], positions)], axis=-1)
    k = jnp.concatenate([k[..., :NOPE_D], rope(k[..., NOPE_D:], positions)], axis=-1)

    scale = 1.0 / math.sqrt(QK_D)
    kh = k.transpose(0, 2, 1, 3).astype(jnp.float32)
    vh = v.transpose(0, 2, 1, 3).astype(jnp.float32)
    nb = S // Q_BLOCK
    qb = q.transpose(0, 2, 1, 3).reshape(B, MLA_HEADS, nb, Q_BLOCK, QK_D).transpose(2, 0, 1, 3, 4)
    key_idx = jnp.arange(S)

    def attend(args):
        qblk, bi = args
        s = jnp.einsum('bhqd,bhkd->bhqk', qblk.astype(jnp.float32), kh) * scale
        q_idx = bi * Q_BLOCK + jnp.arange(Q_BLOCK)
        mask = key_idx[None, :] <= q_idx[:, None]
        s = jnp.where(mask, s, -jnp.inf)
        p = jax.nn.softmax(s, axis=-1)
        return jnp.einsum('bhqk,bhkv->bhqv', p, vh).astype(dt)

    o = lax.map(attend, (qb, jnp.arange(nb)))
    return o.transpose(1, 0, 3, 2, 4).reshape(B, S, MLA_VW)


def setup_inputs(seed: int = 0) -> dict:
    key = jax.random.key(seed)
    ks = jax.random.split(key, 20)
    f32 = jnp.float32

    def nrm(k, shape, fan_in):
        return jax.random.normal(k, shape, f32) * (fan_in ** -0.5)

    def gain(k, shape):
        return 1.0 + 0.01 * jax.random.normal(k, shape, f32)

    x = jax.random.normal(ks[0], (BATCH, SEQ, D_MODEL), f32)
    positions = (jnp.arange(SEQ, dtype=jnp.int32)[None, :]
                 + jax.random.randint(ks[1], (BATCH, 1), 0, 1024, dtype=jnp.int32))
    return {
        "x": x,
        "positions": positions,
        "norm1_w": gain(ks[2], (DEPTH, D_MODEL)),
        "w_in": nrm(ks[3], (DEPTH, D_MODEL, IN_W), D_MODEL),
        "hgrn_lb": 0.1 * jax.random.normal(ks[4], (DEPTH + 1, HG_KW), f32),
        "hgrn_out_norm_w": gain(ks[5], (DEPTH, HG_DV)),
        "mla_q_norm_w": gain(ks[6], (DEPTH, Q_RANK)),
        "w_uq": nrm(ks[7], (DEPTH, Q_RANK, MLA_HEADS * QK_D), Q_RANK),
        "mla_kv_norm_w": gain(ks[8], (DEPTH, KV_RANK)),
        "w_ukv": nrm(ks[9], (DEPTH, KV_RANK, MLA_HEADS * (NOPE_D + V_D)), KV_RANK),
        "q_head_norm_w": gain(ks[10], (DEPTH, QK_D)),
        "k_head_norm_w": gain(ks[11], (DEPTH, QK_D)),
        "w_o": nrm(ks[12], (DEPTH, MIX_W, D_MODEL), MIX_W),
        "norm2_w": gain(ks[13], (DEPTH, D_MODEL)),
        "w_up": nrm(ks[14], (DEPTH, D_MODEL, D_FF), D_MODEL),
        "w_down": nrm(ks[15], (DEPTH, D_FF, D_MODEL), D_FF),
    }


def reference(x, positions, norm1_w, w_in, hgrn_lb, hgrn_out_norm_w, mla_q_norm_w,
              w_uq, mla_kv_norm_w, w_ukv, q_head_norm_w, k_head_norm_w, w_o,
              norm2_w, w_up, w_down):
    lb_all = jnp.cumsum(jax.nn.softmax(hgrn_lb.astype(jnp.float32), axis=0), axis=0)
    split_pts = list(np.cumsum(IN_SPLITS)[:-1])
    h = x
    for l in range(DEPTH):
        n1 = rms_norm(h, norm1_w[l])
        proj = n1 @ w_in[l]
        q_hg, f_hg, i_hg, g_hg, c_q, c_kv, k_rope = jnp.split(proj, split_pts, axis=-1)
        o_hg = hgrn2_mixer(q_hg, f_hg, i_hg, g_hg, lb_all[l], hgrn_out_norm_w[l])
        o_mla = mla_mixer(c_q, c_kv, k_rope, positions, mla_q_norm_w[l], w_uq[l],
                          mla_kv_norm_w[l], w_ukv[l], q_head_norm_w[l], k_head_norm_w[l])
        mixed = jnp.concatenate([o_hg, o_mla], axis=-1)
        h = h + mixed @ w_o[l]
        n2 = rms_norm(h, norm2_w[l])
        hid = jnp.square(jax.nn.relu(n2 @ w_up[l]))
        h = h + hid @ w_down[l]
    return h
```

```python
import math
import numpy as np
import concourse.bass as bass
import concourse.mybir as mybir
from concourse.bass_utils import run_bass_kernel_spmd

F32 = mybir.dt.float32
BF16 = mybir.dt.bfloat16
I32 = mybir.dt.int32
AF = mybir.ActivationFunctionType
ALU = mybir.AluOpType
EPS = 1e-6
ARENA_ELEMS = 106400


class Rec:
    ENGS = ("pe", "act", "dve", "pool", "sp")

    def __init__(self):
        self.ops = []
        self.lw = {}
        self.rd = {}
        self.barrier_deps = {e: set() for e in self.ENGS}

    def add(self, eng, fn, r=(), w=(), dma=None, inc=16):
        i = len(self.ops)
        deps = {}
        for t in r:
            if t in self.lw:
                deps[self.lw[t]] = "raw"
        for t in w:
            for q in self.rd.get(t, ()):
                deps.setdefault(q, "war")
            if t in self.lw:
                deps.setdefault(self.lw[t], "waw")
        for d in self.barrier_deps[eng]:
            deps[d] = "raw"
        self.barrier_deps[eng] = set()
        for t in r:
            self.rd.setdefault(t, []).append(i)
        for t in w:
            self.lw[t] = i
            self.rd[t] = []
        self.ops.append(dict(eng=eng, fn=fn, deps=deps, dma=dma, inc=inc))
        return i

    def barrier(self, skip=()):
        last = {}
        for i, o in enumerate(self.ops):
            if o["dma"] is not None:
                if o["dma"] in skip:
                    continue
                last[("dma", o["dma"])] = i
            else:
                last[o["eng"]] = i
        s = set(last.values())
        for e in self.ENGS:
            self.barrier_deps[e] = set(s)

    def emit(self, block, sems, dma_sems):
        ops = self.ops
        need = set()
        for o in ops:
            for d, kind in o["deps"].items():
                po = ops[d]
                if po["dma"] is not None:
                    continue
                if po["eng"] == o["eng"] and kind != "raw":
                    continue
                need.add(d)
        cnt = {e: 0 for e in self.ENGS}
        dcnt = {}
        for i, o in enumerate(ops):
            if o["dma"] is not None:
                k = o["dma"]
                dcnt[k] = dcnt.get(k, 0) + o["inc"]
                o["sig"] = ("d", k, dcnt[k])
            elif i in need:
                cnt[o["eng"]] += 1
                o["sig"] = ("c", o["eng"], cnt[o["eng"]])
            else:
                o["sig"] = None
        per = {e: [] for e in self.ENGS}
        for i, o in enumerate(ops):
            per[o["eng"]].append(i)

        def run(engname, eng):
            waited = {}
            for i in per[engname]:
                o = ops[i]
                wl = {}
                for d, kind in o["deps"].items():
                    po = ops[d]
                    if po["dma"] is None and po["eng"] == engname and kind != "raw":
                        continue
                    s = po["sig"]
                    if s is None:
                        continue
                    key = (s[0], s[1])
                    wl[key] = max(wl.get(key, 0), s[2])
                for key, val in wl.items():
                    if waited.get(key, 0) >= val:
                        continue
                    waited[key] = val
                    sem = dma_sems[key[1]] if key[0] == "d" else sems[key[1]]
                    eng.wait_ge(sem, val)
                ins = o["fn"](eng)
                s = o["sig"]
                if s is not None:
                    if s[0] == "d":
                        ins.then_inc(dma_sems[s[1]], o["inc"])
                    else:
                        ins.then_inc(sems[s[1]], 1)

        block.tensor(lambda e: run("pe", e))
        block.scalar(lambda e: run("act", e))
        block.vector(lambda e: run("dve", e))
        block.gpsimd(lambda e: run("pool", e))
        block.sync(lambda e: run("sp", e))

    def dma_keys(self):
        ks = []
        for o in self.ops:
            if o["dma"] is not None and o["dma"] not in ks:
                ks.append(o["dma"])
        return ks


class Cfg:
    def __init__(self, D=4096, S=8192, DFF=16384):
        self.D = D
        self.S = S
        self.DFF = DFF
        self.KT = D // 128
        self.NT = S // 128
        self.S4 = S // 4
        self.TB = min(512, self.S4)
        self.NB = self.S4 // self.TB
        self.TPB = self.TB // 128
        nff = DFF // 128
        self.FH = 64 if nff >= 128 else nff // 2
        self.NH = nff // self.FH


def build(cfg, mode="fused", debug=None):
    D, S, DFF, KT, NT, S4 = cfg.D, cfg.S, cfg.DFF, cfg.KT, cfg.NT, cfg.S4
    TB, NB, TPB, FH, NH = cfg.TB, cfg.NB, cfg.TPB, cfg.FH, cfg.NH
    import os as _os2
    do1 = mode in ("fused", "p1")
    do2 = mode in ("fused", "p2")
    nc = bass.Bass("TRN2", target_bir_lowering=False)
    R = Rec()

    def din(name, shape, dt=F32):
        return nc.dram_tensor(name, list(shape), dt, kind="ExternalInput").ap()

    def dscr(name, shape, dt):
        return nc.dram_tensor(name, list(shape), dt).ap()

    def dout(name, shape, dt):
        return nc.dram_tensor(name, list(shape), dt, kind="ExternalOutput").ap()

    dbg = {}
    if debug:
        for nm, shp in debug.items():
            dbg[nm] = dout(nm, shp, F32)

    ctx = []
    t = nc.sbuf_tensor("arena", [128, ARENA_ELEMS], BF16)
    ctx.append(t)
    ARENA = t.__enter__()
    PB = []
    for i in range(8):
        t = nc.psum_tensor(f"pb{i}", [128, 512], F32)
        ctx.append(t)
        PB.append(t.__enter__())

    class Carver:
        def __init__(self, start=0):
            self.off = start

        def b(self, n):
            o = self.off
            self.off += n + (n & 1)
            assert self.off <= ARENA_ELEMS, self.off
            return ARENA[:, o:o + n]

        def f(self, n, dt=F32):
            o = self.off
            self.off += 2 * n
            assert self.off <= ARENA_ELEMS, self.off
            return ARENA[:, o:o + 2 * n].bitcast(dt)

    CV = Carver(0)
    IDB = CV.b(128)
    ones_b = CV.b(128)
    tri_in = CV.f(128)
    tri_af = CV.f(128)
    ind2 = CV.f(2)
    pi_t = CV.f(2)[:, 0:1]
    npi_t = CV.f(2)[:, 0:1]
    SEL = CV.f(4)
    NOWBC_END = CV.off
    WBC = CV.f(D)
    PERSIST_END = CV.off

    def _consts(e):
        e.memset(IDB, 0.0)
        e.affine_select(out=IDB, in_=IDB, pattern=[[-1, 128]], compare_op=ALU.not_equal,
                        fill=1.0, base=0, channel_multiplier=1)
        e.memset(tri_in, 1.0)
        e.affine_select(out=tri_in, in_=tri_in, pattern=[[1, 128]], compare_op=ALU.is_ge,
                        fill=0.0, base=0, channel_multiplier=-1)
        e.memset(tri_in[0:64, 64:128], 0.0)
        e.memset(tri_af, 1.0)
        e.affine_select(out=tri_af, in_=tri_af, pattern=[[-1, 128]], compare_op=ALU.is_gt,
                        fill=0.0, base=0, channel_multiplier=1)
        e.memset(tri_af[64:128, 0:64], 0.0)
        e.memset(ind2, 0.0)
        e.memset(ind2[0:64, 0:1], 1.0)
        e.memset(ind2[64:128, 1:2], 1.0)
        e.memset(pi_t, math.pi)
        e.memset(npi_t, -math.pi)
        return e.memset(ones_b, 1.0)

    R.add("pool", _consts, w=["consts"])

    def DMA(q, out_ap, in_ap, key, r=(), w=()):
        return R.add(q, lambda e: e.dma_start(out=out_ap, in_=in_ap), r=r, w=w, dma=key)

    def MM(out_ap, lhsT, rhs, start, stop, r=(), w=(), tp=None):
        if tp is None:
            return R.add("pe", lambda e: e.matmul(out_ap, lhsT=lhsT, rhs=rhs, start=start, stop=stop), r=r, w=w)
        return R.add("pe", lambda e: e.matmul(out_ap, lhsT=lhsT, rhs=rhs, start=start, stop=stop,
                                              tile_position=tp), r=r, w=w)

    def TR(out_ap, in_ap, r=(), w=()):
        n = in_ap.shape[0]
        return R.add("pe", lambda e: e.transpose(out_ap, in_ap, IDB[0:n, 0:n]), r=list(r) + ["consts"], w=w)

    def ACT(out_ap, in_ap, func, r=(), w=(), bias=None, scale=None, accum=None):
        kw = {}
        if bias is not None:
            kw["bias"] = bias
        if scale is not None:
            kw["scale"] = scale
        if accum is not None:
            kw["accum_out"] = accum
        return R.add("act", lambda e: e.activation(out=out_ap, in_=in_ap, func=func, **kw), r=r, w=w)

    def TS(eng, out_ap, in0, s1, s2, op0, op1=None, r=(), w=()):
        if op1 is None:
            return R.add(eng, lambda e: e.tensor_scalar(out=out_ap, in0=in0, scalar1=s1, scalar2=None, op0=op0), r=r, w=w)
        return R.add(eng, lambda e: e.tensor_scalar(out=out_ap, in0=in0, scalar1=s1, scalar2=s2, op0=op0, op1=op1), r=r, w=w)

    def STT(eng, out_ap, in0, scalar, in1, op0, op1, r=(), w=()):
        return R.add(eng, lambda e: e.scalar_tensor_tensor(out=out_ap, in0=in0, scalar=scalar, in1=in1,
                                                           op0=op0, op1=op1), r=r, w=w)

    def TT(eng, out_ap, in0, in1, op, r=(), w=()):
        return R.add(eng, lambda e: e.tensor_tensor(out=out_ap, in0=in0, in1=in1, op=op), r=r, w=w)

    def CP(eng, out_ap, in_ap, r=(), w=()):
        if eng == "act":
            return R.add("act", lambda e: e.activation(out=out_ap, in_=in_ap, func=AF.Copy), r=r, w=w)
        return R.add(eng, lambda e: e.tensor_copy(out=out_ap, in_=in_ap), r=r, w=w)

    def rstd_from_ss(rs, ss, n, r, w):
        TS("dve", rs, ss, 1.0 / n, EPS, ALU.mult, ALU.add, r=r, w=w)
        ACT(rs, rs, AF.Sqrt, r=w, w=w)
        R.add("dve", lambda e: e.reciprocal(out=rs, in_=rs), r=w, w=w)

    def DBG(name, src_ap, r):
        if name in dbg:
            DMA("sp", dbg[name], src_ap, "dbg_" + name, r=r, w=["dbg_" + name])

    def norm_transpose(src_rows, slot, XT, XN, SSR, wtok, dst_fn, dst_tok, rd_tok=(), uid=""):
        xt = XT[slot]
        xtk = f"xt{slot}"
        DMA("sp", xt, src_rows, xtk, r=list(rd_tok), w=[xtk])
        ss = SSR[:, slot:slot + 1]
        rs = SSR[:, 2 + slot:3 + slot]
        XNs = XN[slot]
        xnk = f"xn{slot}"
        ACT(XN[2], xt, AF.Square, r=[xtk], w=["xnjunk", f"ss{slot}"], accum=ss)
        rstd_from_ss(rs, ss, D, r=[f"ss{slot}"], w=[f"rs{slot}"])
        STT("dve", XNs, xt, rs, WBC, ALU.mult, ALU.mult,
            r=[xtk, f"rs{slot}", wtok], w=[xnk])
        ngrp = (KT + 7) // 8
        for g in range(ngrp):
            bi = 4 + (g % 4)
            pbf = PB[bi][:].bitcast(BF16)
            nk = min(8, KT - g * 8)
            for kk in range(nk):
                k = g * 8 + kk
                TR(pbf[:, kk * 128:(kk + 1) * 128], XNs[:, k * 128:(k + 1) * 128],
                   r=[xnk], w=[f"pb{bi}"])
            src = pbf[:, 0:nk * 128].rearrange("p (k t) -> p k t", k=nk)
            CP("act" if g % 2 == 0 else "dve", dst_fn(g, nk), src, r=[f"pb{bi}"], w=[dst_tok])

    def precast(part):
        if part == "B":
            for g in range(DFF // (128 * SUB)):
                for c in range(NCW):
                    DMA("pool", wdn_t[c, g].rearrange("p (j n) -> p j n", j=SUB),
                        w_dn[g * SUB * 128:(g + 1) * SUB * 128, c * CW:(c + 1) * CW].rearrange("(j p) n -> p j n", p=128),
                        "precast", w=["wdn_b"])
            return
        for c in range(NCW):
            DMA("pool", wo_t[c].rearrange("p (k n) -> p k n", k=32),
                w_o[:, c * CW:(c + 1) * CW].rearrange("(k p) n -> p k n", p=128), "precast", w=["wo_b"])
        for k in range(KT):
            DMA("pool", wup_t[:, :, k * 256:(k + 1) * 256].rearrange("s p n -> p s n"),
                w_up[k * 128:(k + 1) * 128, :].rearrange("p (s n) -> p s n", n=256), "precast", w=["wup_b"])

    def phase2():
        nonlocal_uid = [0]
        DMA("sp", WBC, n2w[0:1, :].broadcast_to([128, D]), "wbc", w=["wbc2"])
        if mode == "fused":
            DMA("sp", SEL, sel[0:1, :].broadcast_to([128, 4]), "sel", w=["sel"])
        else:
            mixv = mix_all.rearrange("s (r p) t -> p (s r) t", p=128)
        NC = NCW
        for tb in range(NB):
            t0 = tb * TB
            R.barrier(skip=("precast",))
            C = Carver(PERSIST_END)
            MIXT = C.b(32 * TB).rearrange("p (k t) -> p k t", k=32)
            WO = [C.b(32 * CW).rearrange("p (k n) -> p k n", k=32) for _ in range(2)]
            RIN = [C.f(TPB * CW).rearrange("p (i n) -> p i n", i=TPB) for _ in range(2)]
            ROUT = [C.f(TPB * CW).rearrange("p (i n) -> p i n", i=TPB) for _ in range(2)]
            u = f"b{tb}"
            if mode == "fused":
                STG = [C.b(8 * TB).rearrange("p (r t) -> p r t", r=8) for _ in range(2)]
                si = 0
                for s_ in range(4):
                    dstv = MIXT[:, s_ * 8:(s_ + 1) * 8, :]
                    for d in range(4):
                        sg = si % 2
                        si += 1
                        rpq = CR // 128
                        for q in range(NQ):
                            DMA("sp", STG[sg][:, q * rpq:(q + 1) * rpq, :],
                                gathered[d, q, s_, :, t0:t0 + TB].rearrange("(r p) t -> p r t", p=128),
                                f"stg{sg}", r=["mix_all"], w=[f"stg{sg}"])
                        if d == 0:
                            TS("dve", dstv, STG[sg], SEL[:, 0:1], None, ALU.mult,
                               r=[f"stg{sg}", "sel"], w=[u + "mixt"])
                        else:
                            STT("dve", dstv, STG[sg], SEL[:, d:d + 1], dstv, ALU.mult, ALU.add,
                                r=[f"stg{sg}", "sel", u + "mixt"], w=[u + "mixt"])
            else:
                DMA("sp", MIXT, mixv[:, :, t0:t0 + TB], "mixt", r=["mix_all"], w=[u + "mixt"])
            for c in range(NC):
                sl = c % 2
                DMA("sp", WO[sl], wo_t[c].rearrange("p (k n) -> p k n", k=32), f"wo{sl}", r=["wo_b"], w=[f"wo{sl}"])
                DMA("sp", RIN[sl], x_res[t0:t0 + TB, c * CW:(c + 1) * CW].rearrange("(i p) n -> p i n", p=128),
                    f"rin{sl}", w=[f"rin{sl}"])
                for ti in range(TPB):
                    bi = (c % 2) * 4 + ti
                    for k in range(32):
                        MM(PB[bi][:, 0:CW], MIXT[:, k, ti * 128:(ti + 1) * 128], WO[sl][:, k, :],
                           k == 0, k == 31, r=[u + "mixt", f"wo{sl}"], w=[f"pb{bi}"])
                    TT("dve", ROUT[sl][:, ti, :], PB[bi][:, 0:CW], RIN[sl][:, ti, :], ALU.add,
                       r=[f"pb{bi}", f"rin{sl}"], w=[f"rout{sl}"])
                DMA("sp", hbuf[t0:t0 + TB, c * CW:(c + 1) * CW].rearrange("(i p) n -> p i n", p=128),
                    ROUT[sl], f"rout{sl}", r=[f"rout{sl}"], w=[f"hbuf{tb}_{c}"])
            R.barrier(skip=("precast",))
            C = Carver(PERSIST_END)
            N2T = C.b(KT * TB).rearrange("p (k t) -> p k t", k=KT)
            HID0 = C.off
            XT = [C.f(D) for _ in range(2)]
            XN = [C.b(D) for _ in range(3)]
            SSR = C.f(4)
            for ti in range(TPB):
                norm_transpose(hbuf[t0 + ti * 128:t0 + (ti + 1) * 128, :], ti % 2, XT, XN, SSR, "wbc2",
                               lambda g, nk, ti=ti: N2T[:, g * 8:g * 8 + nk, ti * 128:(ti + 1) * 128],
                               u + "n2t", rd_tok=[f"hbuf{tb}_{c}" for c in range(NC)])
            R.barrier(skip=("precast",))
            C = Carver(HID0)
            HIDT = C.b(FH * TB).rearrange("p (j t) -> p j t", j=FH)
            NWB = 3
            WB = [C.b(4096 * 2) for _ in range(NWB)]
            RIN = [C.f(TPB * CW).rearrange("p (i n) -> p i n", i=TPB) for _ in range(2)]
            ROUT = [C.f(TPB * CW).rearrange("p (i n) -> p i n", i=TPB) for _ in range(2)]
            RELU = [C.f(TB) for _ in range(2)]
            wi = 0
            for hf in range(NH):
                uh = f"{u}h{hf}"
                for sidx in range(FH // 2):
                    col0 = (hf * FH + sidx * 2) * 128
                    sl = wi % NWB
                    wi += 1
                    wsl = WB[sl][:, 0:KT * 256].rearrange("p (k n) -> p k n", k=KT)
                    DMA("sp", WB[sl][:, 0:KT * 256], wup_t[col0 // 256], f"wb{sl}", r=["wup_b"], w=[f"wb{sl}"])
                    for j in range(2):
                        fc = sidx * 2 + j
                        bi = fc % 4
                        for k in range(KT):
                            MM(PB[bi][:, 0:TB], wsl[:, k, j * 128:(j + 1) * 128], N2T[:, k, :],
                               k == 0, k == KT - 1, r=[f"wb{sl}", u + "n2t"], w=[f"pb{bi}"])
                        rl = RELU[fc % 2]
                        ACT(rl, PB[bi][:, 0:TB], AF.Relu, r=[f"pb{bi}"], w=[f"relu{fc % 2}"])
                        TT("dve", HIDT[:, fc, :], rl, rl, ALU.mult,
                           r=[f"relu{fc % 2}"], w=[f"{uh}hid{fc}"])
                last = hf == NH - 1
                for c in range(NC):
                    sl2 = c % 2
                    for sub in range(FH // SUB):
                        sl = wi % NWB
                        wi += 1
                        wsl = WB[sl][:, 0:SUB * CW].rearrange("p (j n) -> p j n", j=SUB)
                        DMA("sp", WB[sl][:, 0:SUB * CW], wdn_t[c, (hf * FH) // SUB + sub],
                            f"wb{sl}", r=["wdn_b"], w=[f"wb{sl}"])
                        for j in range(SUB):
                            fc = sub * SUB + j
                            for ti in range(TPB):
                                bi = (c % 2) * 4 + ti
                                MM(PB[bi][:, 0:CW], HIDT[:, fc, ti * 128:(ti + 1) * 128], wsl[:, j, :],
                                   fc == 0, fc == FH - 1, r=[f"wb{sl}", f"{uh}hid{fc}"], w=[f"pb{bi}"])
                    DMA("sp", RIN[sl2], hbuf[t0:t0 + TB, c * CW:(c + 1) * CW].rearrange("(i p) n -> p i n", p=128),
                        f"rin{sl2}", r=[f"hbuf{tb}_{c}"], w=[f"rin{sl2}"])
                    for ti in range(TPB):
                        bi = (c % 2) * 4 + ti
                        TT("dve", ROUT[sl2][:, ti, :], PB[bi][:, 0:CW], RIN[sl2][:, ti, :], ALU.add,
                           r=[f"pb{bi}", f"rin{sl2}"], w=[f"rout{sl2}"])
                    dst = out if last else hbuf
                    DMA("sp", dst[t0:t0 + TB, c * CW:(c + 1) * CW].rearrange("(i p) n -> p i n", p=128),
                        ROUT[sl2], f"rout{sl2}", r=[f"rout{sl2}"], w=[f"hbuf{tb}_{c}"])


    def phase1a():
        R.barrier(skip=("precast",))
        DMA("sp", WBC, n1w[0:1, :].broadcast_to([128, D]), "wbc", w=["wbc1"])
        C = Carver(PERSIST_END)
        XT = [C.f(D) for _ in range(2)]
        XN = [C.b(D) for _ in range(3)]
        SSR = C.f(4)
        XNT = [C.b(KT * 128) for _ in range(2)]
        for i in range(NT):
            sl = i % 2
            v3 = XNT[sl].rearrange("p (k t) -> p k t", k=KT)
            norm_transpose(x_seq[i * 128:(i + 1) * 128, :], sl, XT, XN, SSR, "wbc1",
                           lambda g, nk, v3=v3: v3[:, g * 8:g * 8 + nk, :], f"xnt{sl}")
            DMA("sp", xnT_d[i], XNT[sl], f"xnts{sl}", r=[f"xnt{sl}"], w=[f"xnTd{i}"])

    def phase1b():
        R.barrier(skip=("precast",))
        C = Carver(NOWBC_END)
        WHG = C.b(KT * 2048).rearrange("p (k n) -> p k n", k=KT)
        XNT = [C.b(KT * 128) for _ in range(2)]
        sgn, kk, logf, eb, enb, er, qs, gate, on = [C.f(512) for _ in range(9)]
        q_dec, k_inv, k_tail, v_b, og, AT_b, state_b, junk = [C.b(512) for _ in range(8)]
        QKT = C.b(1024)
        q_decT, k_invT = QKT[:, 0:512], QKT[:, 512:1024]
        state_m = C.b(512)
        state_f = C.f(512)
        oml = C.f(512)
        lb1 = C.f(512)
        atmask = C.f(512)
        hgw = C.f(128)
        decT = C.f(8)
        oss = C.f(4)
        rso = C.f(4)
        one_t = C.f(2)[:, 0:1]
        OSB = min(512, S4)
        OSTN = OSB // 128
        OST = [C.b(4 * OSB).rearrange("p (h t) -> p h t", h=4) for _ in range(2)]
        kc = max(1, KT // 8)
        for g in range(KT // kc):
            DMA("pool", WHG[:, g * kc:(g + 1) * kc, :],
                w_hg[g * kc * 128:(g + 1) * kc * 128, :].rearrange("(k p) n -> p k n", p=128),
                "whg", w=["whg"])
        DMA("sp", oml, lb_raw[0:1, :].broadcast_to([128, 512]), "lb0", w=["oml"])
        DMA("sp", lb1, lb_raw[1:2, :].broadcast_to([128, 512]), "lb1", w=["lb1"])
        DMA("sp", hgw, hgnw[0:1, :].broadcast_to([128, 128]), "hgw", w=["hgw"])
        TT("dve", oml, lb1, oml, ALU.subtract, r=["lb1", "oml"], w=["oml"])
        ACT(oml, oml, AF.Sigmoid, r=["oml"], w=["oml"])
        for h in range(4):
            CP("pool", atmask[:, h * 128:(h + 1) * 128], tri_in, r=["consts"], w=["atmask"])
        R.add("pool", lambda e: e.memset(state_f, 0.0), w=["state_f"])
        R.add("pool", lambda e: e.memset(state_b, 0.0), w=["state_b"])
        R.add("pool", lambda e: e.memset(one_t, 1.0), w=["one_t"])
        pb6 = PB[6][:].bitcast(BF16)
        def inproj_ops(i):
            sl = i % 2
            xk = f"hxnt{sl}"
            ops_ = [lambda: DMA("sp", XNT[sl], xnT_d[i], xk, r=[f"xnTd{i}"], w=[xk])]
            for cg in (1, 0, 2, 3):
                for k in range(KT):
                    ops_.append(lambda cg=cg, k=k: MM(
                        PB[cg][:], XNT[sl][:, k * 128:(k + 1) * 128], WHG[:, k, cg * 512:(cg + 1) * 512],
                        k == 0, k == KT - 1, r=[xk, "whg"], w=[f"pb{cg}"]))
            return ops_

        pend = inproj_ops(0)
        for f_ in pend:
            f_()
        for i in range(NT):
            nxt = inproj_ops(i + 1) if i + 1 < NT else []
            npc = (len(nxt) + 4) // 5

            def piece(j):
                for f_ in nxt[j * npc:(j + 1) * npc]:
                    f_()
            ACT(sgn, PB[1][:], AF.Sigmoid, r=["pb1"], w=["sgn"], scale=-1.0)
            ACT(qs, PB[0][:], AF.Silu, r=["pb0"], w=["qs"])
            TT("dve", kk, sgn, oml, ALU.mult, r=["sgn", "oml"], w=["kk"])
            ACT(logf, kk, AF.Ln, r=["kk", "one_t"], w=["logf"], scale=-1.0, bias=one_t)
            MM(PB[4][:], tri_in, logf, True, True, r=["logf", "consts"], w=["pb4"])
            MM(PB[5][:], tri_af, logf, True, True, r=["logf", "consts"], w=["pb5"])
            for h in range(4):
                MM(PB[7][:, 2 * h:2 * h + 2], logf[:, h * 128:(h + 1) * 128], ind2, True, True,
                   r=["logf", "consts"], w=["pb7"])
            ACT(gate, PB[3][:], AF.Silu, r=["pb3"], w=["gate"])
            CP("dve", v_b, PB[2][:], r=["pb2"], w=["v_b"])
            piece(0)
            ACT(eb, PB[4][:], AF.Exp, r=["pb4"], w=["eb"])
            ACT(enb, PB[4][:], AF.Exp, r=["pb4"], w=["enb"], scale=-1.0)
            ACT(er, PB[5][:], AF.Exp, r=["pb5"], w=["er"])
            ACT(decT, PB[7][:, 0:8], AF.Exp, r=["pb7"], w=["decT"])
            TT("dve", q_dec, qs, eb, ALU.mult, r=["qs", "eb"], w=["q_dec"])
            TT("dve", k_inv, kk, enb, ALU.mult, r=["kk", "enb"], w=["k_inv"])
            TT("pool", k_tail, kk, er, ALU.mult, r=["kk", "er"], w=["k_tail"])
            for h in range(4):
                TR(pb6[:, h * 128:(h + 1) * 128], q_dec[:, h * 128:(h + 1) * 128], r=["q_dec"], w=["pb6"])
            for h in range(4):
                TR(pb6[:, 512 + h * 128:512 + (h + 1) * 128], k_inv[:, h * 128:(h + 1) * 128], r=["k_inv"], w=["pb6"])
            CP("act" if i % 2 == 0 else "dve", QKT, pb6[:, 0:1024], r=["pb6"], w=["q_decT", "k_invT"])
            piece(1)
            for h in range(4):
                hc = slice(h * 128, (h + 1) * 128)
                MM(PB[4][:, hc], k_invT[:, hc], q_decT[:, hc], True, True, r=["k_invT", "q_decT"], w=["pb4"])
            TT("dve", AT_b, PB[4][:], atmask, ALU.mult, r=["pb4", "atmask"], w=["AT_b"])
            piece(2)
            def state_update(ch, dst_b, dst_tok):
                ps_ = slice(ch * 64, (ch + 1) * 64)
                for h in range(4):
                    hc = slice(h * 128, (h + 1) * 128)
                    MM(PB[7][:, hc], k_tail[ps_, hc], v_b[ps_, hc], True, True, r=["k_tail", "v_b"], w=["pb7"])
                for h in range(4):
                    hc = slice(h * 128, (h + 1) * 128)
                    STT("dve", state_f[:, hc], state_f[:, hc], decT[:, 2 * h + ch:2 * h + ch + 1], PB[7][:, hc],
                        ALU.mult, ALU.add, r=["state_f", "decT", "pb7"], w=["state_f"])
                CP("pool", dst_b, state_f, r=["state_f"], w=[dst_tok])

            state_update(0, state_m, "state_m")
            for h in range(4):
                hc = slice(h * 128, (h + 1) * 128)
                MM(PB[5][:, hc], AT_b[:, hc], v_b[:, hc], True, False, r=["AT_b", "v_b"], w=["pb5"])
                MM(PB[5][0:64, hc], q_decT[:, h * 128:h * 128 + 64], state_b[:, hc], False, False,
                   r=["q_decT", "state_b"], w=["pb5"])
                MM(PB[5][64:128, hc], q_decT[:, h * 128 + 64:h * 128 + 128], state_m[:, hc], False, True,
                   r=["q_decT", "state_m"], w=["pb5"], tp=(0, 64))
            piece(3)
            state_update(1, state_b, "state_b")
            for h in range(4):
                hc = slice(h * 128, (h + 1) * 128)
                ACT(junk[:, hc], PB[5][:, hc], AF.Square, r=["pb5"], w=["junk", "oss"], accum=oss[:, h:h + 1])
            rstd_from_ss(rso, oss, 128, r=["oss"], w=["rso"])
            for h in range(4):
                hc = slice(h * 128, (h + 1) * 128)
                STT("dve", on[:, hc], PB[5][:, hc], rso[:, h:h + 1], hgw, ALU.mult, ALU.mult,
                    r=["pb5", "rso", "hgw"], w=["on"])
            TT("pool", og, on, gate, ALU.mult, r=["on", "gate"], w=["og"])
            piece(4)
            for h in range(4):
                TR(pb6[:, h * 128:(h + 1) * 128], og[:, h * 128:(h + 1) * 128], r=["og"], w=["pb6"])
            osl = (i // OSTN) % 2
            ti = i % OSTN
            CP("act", OST[osl][:, :, ti * 128:(ti + 1) * 128],
               pb6[:, 0:512].rearrange("p (h t) -> p h t", h=4), r=["pb6"], w=[f"ost{osl}"])
            if ti == OSTN - 1:
                T0 = (i // OSTN) * OSB
                dq, tq0 = T0 // S4, T0 % S4
                DMA("sp", a2a_in[dq, 0:512, tq0:tq0 + OSB].rearrange("(h p) t -> p h t", p=128), OST[osl],
                    f"ost{osl}", r=[f"ost{osl}"], w=["a2a_in"])
            if i == 0:
                DBG("dbg_o", on, ["on"])

    def phase1c():
        R.barrier(skip=("precast",))
        C = Carver(NOWBC_END)
        WML = C.b(KT * 1344).rearrange("p (k n) -> p k n", k=KT)
        WUQ = C.b(6 * 768).rearrange("p (k n) -> p k n", k=6)
        WUKV = C.b(4 * 1024).rearrange("p (k n) -> p k n", k=4)
        XNT = [C.b(KT * 128) for _ in range(2)]
        qnw_bc = C.f(768)
        kvnw_bc = C.f(512)
        qhw_bc = C.f(192)
        khw_bc = C.f(192)
        cos_all = C.f(NT * 32)
        sin_all = C.f(NT * 32)
        ang = C.f(NT * 32)
        posf = C.f(NT)
        posT = C.f(128)
        posTi = C.f(128, I32)
        invf_bc = C.f(32)
        identF = C.f(128)
        cqn = C.b(768)
        ckvn = C.b(512)
        cqnT = C.b(768)
        ckvnT = C.b(512)
        junk = C.b(768)
        krope = C.f(64)
        qf = C.f(768).rearrange("p (h d) -> p h d", h=4)
        kf = C.f(768).rearrange("p (h d) -> p h d", h=4)
        qb = C.b(768).rearrange("p (h d) -> p h d", h=4)
        kb = C.b(768).rearrange("p (h d) -> p h d", h=4)
        vb = C.b(512)
        tmp = [C.f(128).rearrange("p (h d) -> p h d", h=4) for _ in range(4)]
        tmpk = [C.f(128).rearrange("p (h d) -> p h d", h=4) for _ in range(4)]
        QKn_s = C.b(1024)
        QTn_s, KTn_s = QKn_s[:, 0:512], QKn_s[:, 512:1024]
        QTr_s = C.b(512)
        KTr_s = C.b(512)
        st = C.f(32)
        kc = max(1, KT // 8)
        for g in range(KT // kc):
            DMA("pool", WML[:, g * kc:(g + 1) * kc, :],
                w_ml[g * kc * 128:(g + 1) * kc * 128, :].rearrange("(k p) n -> p k n", p=128), "wml", w=["wml"])
        DMA("pool", WUQ, w_uq.rearrange("(k p) n -> p k n", p=128), "wuq", w=["wuq"])
        DMA("pool", WUKV, w_ukv.rearrange("(k p) n -> p k n", p=128), "wukv", w=["wukv"])
        DMA("sp", qnw_bc, qnw[0:1, :].broadcast_to([128, 768]), "c1", w=["qnw"])
        DMA("sp", kvnw_bc, kvnw[0:1, :].broadcast_to([128, 512]), "c2", w=["kvnw"])
        DMA("sp", qhw_bc, qhw[0:1, :].broadcast_to([128, 192]), "c3", w=["qhw"])
        DMA("sp", khw_bc, khw[0:1, :].broadcast_to([128, 192]), "c4", w=["khw"])
        DMA("sp", invf_bc, invf[0:1, :].broadcast_to([128, 32]), "c5", w=["invf"])
        DMA("sp", posTi[0:NT, :], pos[0:1, :].rearrange("o (n p) -> (o n) p", p=128), "c6", w=["posTi"])

        def _idf(e):
            e.memset(identF, 0.0)
            return e.affine_select(out=identF, in_=identF, pattern=[[-1, 128]], compare_op=ALU.not_equal,
                                   fill=1.0, base=0, channel_multiplier=1)
        R.add("pool", _idf, w=["identF"])
        CP("dve", posT[0:NT, :], posTi[0:NT, :], r=["posTi"], w=["posT"])
        MM(PB[0][:, 0:NT], posT[0:NT, :], identF[0:NT, 0:NT], True, True, r=["posT", "identF"], w=["pb0"])
        CP("dve", posf, PB[0][:, 0:NT], r=["pb0"], w=["posf"])
        for n in range(NT):
            TS("dve", ang[:, n * 32:(n + 1) * 32], invf_bc, posf[:, n:n + 1], None, ALU.mult,
               r=["invf", "posf"], w=["ang"])
        angi = C.f(NT * 32, I32)
        angn = C.f(NT * 32)
        for (dst, shift, nm) in ((sin_all, 0.5, "sin_all"), (cos_all, 0.75, "cos_all")):
            TS("dve", dst, ang, 1.0 / (2.0 * math.pi), shift, ALU.mult, ALU.add, r=["ang"], w=[nm])
            CP("dve", angi, dst, r=[nm], w=["angi"])
            CP("dve", angn, angi, r=["angi"], w=["angn"])
            TT("dve", dst, dst, angn, ALU.subtract, r=[nm, "angn"], w=[nm])
            TS("dve", angn, dst, 0.0, None, ALU.is_lt, r=[nm], w=["angn"])
            TT("dve", dst, dst, angn, ALU.add, r=[nm, "angn"], w=[nm])
            ACT(dst, dst, AF.Sin, r=[nm, "consts"], w=[nm], scale=2.0 * math.pi, bias=npi_t)
        pb4 = PB[4][:].bitcast(BF16)
        pb5 = PB[5][:].bitcast(BF16)
        import os as _os
        stop = int(_os.environ.get("PH1C_STOP", "9"))
        for i in range(NT if stop > 0 else 0):
            sl = i % 2
            xk = f"mxnt{sl}"
            t0 = i * 128
            DMA("sp", XNT[sl], xnT_d[i], xk, r=[f"xnTd{i}"], w=[xk])
            for (bi, c0, cw) in ((0, 0, 512), (1, 512, 320), (2, 832, 512)):
                for k in range(KT):
                    MM(PB[bi][:, 0:cw], XNT[sl][:, k * 128:(k + 1) * 128], WML[:, k, c0:c0 + cw],
                       k == 0, k == KT - 1, r=[xk, "wml"], w=[f"pb{bi}"])
            ACT(junk[:, 0:512], PB[0][:], AF.Square, r=["pb0"], w=["junk", "st0"], accum=st[:, 0:1])
            ACT(junk[:, 0:256], PB[1][:, 0:256], AF.Square, r=["pb1"], w=["junk", "st1"], accum=st[:, 1:2])
            ACT(junk[:, 0:512], PB[2][:], AF.Square, r=["pb2"], w=["junk", "st3"], accum=st[:, 3:4])
            TT("dve", st[:, 2:3], st[:, 0:1], st[:, 1:2], ALU.add, r=["st0", "st1"], w=["st2"])
            rstd_from_ss(st[:, 4:5], st[:, 2:3], 768, r=["st2"], w=["rsq"])
            rstd_from_ss(st[:, 5:6], st[:, 3:4], 512, r=["st3"], w=["rskv"])
            STT("dve", cqn[:, 0:512], PB[0][:], st[:, 4:5], qnw_bc[:, 0:512], ALU.mult, ALU.mult,
                r=["pb0", "rsq", "qnw"], w=["cqn"])
            STT("dve", cqn[:, 512:768], PB[1][:, 0:256], st[:, 4:5], qnw_bc[:, 512:768], ALU.mult, ALU.mult,
                r=["pb1", "rsq", "qnw"], w=["cqn"])
            CP("dve", krope, PB[1][:, 256:320], r=["pb1", "st1"], w=["krope"])
            STT("dve", ckvn, PB[2][:], st[:, 5:6], kvnw_bc, ALU.mult, ALU.mult,
                r=["pb2", "rskv", "kvnw"], w=["ckvn"])
            for k in range(6):
                TR(pb4[:, k * 128:(k + 1) * 128], cqn[:, k * 128:(k + 1) * 128], r=["cqn"], w=["pb4"])
            for k in range(4):
                TR(pb5[:, k * 128:(k + 1) * 128], ckvn[:, k * 128:(k + 1) * 128], r=["ckvn"], w=["pb5"])
            CP("act", cqnT, pb4[:, 0:768], r=["pb4"], w=["cqnT"])
            CP("dve", ckvnT, pb5[:, 0:512], r=["pb5"], w=["ckvnT"])
            if stop < 2:
                continue
            for half in range(2):
                for k in range(6):
                    MM(PB[6 + half][:, 0:384], cqnT[:, k * 128:(k + 1) * 128], WUQ[:, k, half * 384:(half + 1) * 384],
                       k == 0, k == 5, r=["cqnT", "wuq"], w=[f"pb{6 + half}"])
            for half in range(2):
                for k in range(4):
                    MM(PB[half][:], ckvnT[:, k * 128:(k + 1) * 128], WUKV[:, k, half * 512:(half + 1) * 512],
                       k == 0, k == 3, r=["ckvnT", "wukv"], w=[f"pb{half}"])
            CP("act", vb, PB[1][:], r=["pb1"], w=["vb"])
            DMA("sp", V_d[:, t0:t0 + 128, :].rearrange("h p d -> p h d"), vb.rearrange("p (h d) -> p h d", h=4),
                "vst", r=["vb"], w=["V_d"])
            for h in range(4):
                src = PB[6 + h // 2][:, (h % 2) * 192:(h % 2) * 192 + 192]
                ACT(junk[:, 0:192], src, AF.Square, r=[f"pb{6 + h // 2}"], w=["junk", "ssqh"], accum=st[:, 8 + h:9 + h])
            for h in range(4):
                ACT(junk[:, 0:128], PB[0][:, h * 128:(h + 1) * 128], AF.Square, r=["pb0"], w=["junk", "sskh"],
                    accum=st[:, 12 + h:13 + h])
            ACT(junk[:, 0:64], krope, AF.Square, r=["krope"], w=["junk", "sskr"], accum=st[:, 6:7])
            TS("dve", st[:, 12:16], st[:, 12:16], st[:, 6:7], None, ALU.add, r=["sskh", "sskr"], w=["sskh"])
            rstd_from_ss(st[:, 16:20], st[:, 8:12], 192, r=["ssqh"], w=["rsqh"])
            rstd_from_ss(st[:, 20:24], st[:, 12:16], 192, r=["sskh"], w=["rskh"])
            for h in range(4):
                src = PB[6 + h // 2][:, (h % 2) * 192:(h % 2) * 192 + 192]
                STT("dve", qf[:, h, :], src, st[:, 16 + h:17 + h], qhw_bc, ALU.mult, ALU.mult,
                    r=[f"pb{6 + h // 2}", "rsqh", "qhw"], w=["qf"])
                STT("dve", kf[:, h, 0:128], PB[0][:, h * 128:(h + 1) * 128], st[:, 20 + h:21 + h], khw_bc[:, 0:128],
                    ALU.mult, ALU.mult, r=["pb0", "rskh", "khw"], w=["kf"])
                STT("dve", kf[:, h, 128:192], krope, st[:, 20 + h:21 + h], khw_bc[:, 128:192],
                    ALU.mult, ALU.mult, r=["krope", "rskh", "khw"], w=["kf"])
            if stop < 3:
                continue
            cb = cos_all[:, i * 32:(i + 1) * 32].rearrange("p (o n) -> p o n", o=1).broadcast_to([128, 4, 32])
            sb_ = sin_all[:, i * 32:(i + 1) * 32].rearrange("p (o n) -> p o n", o=1).broadcast_to([128, 4, 32])
            for (src, dst, nm, eng) in ((qf, qb, "q", "pool"), (kf, kb, "k", "dve")):
                x1 = src[:, :, 128:160]
                x2 = src[:, :, 160:192]
                tk = [f"tmp{nm}{j}" for j in range(4)]
                tm = tmp if nm == "q" else tmpk
                TT(eng, tm[0], x1, cb, ALU.mult, r=[nm + "f", "cos_all"], w=[tk[0]])
                TT(eng, tm[1], x2, sb_, ALU.mult, r=[nm + "f", "sin_all"], w=[tk[1]])
                TT(eng, tm[2], x2, cb, ALU.mult, r=[nm + "f", "cos_all"], w=[tk[2]])
                TT(eng, tm[3], x1, sb_, ALU.mult, r=[nm + "f", "sin_all"], w=[tk[3]])
                TT(eng, dst[:, :, 128:160], tm[0], tm[1], ALU.subtract, r=[tk[0], tk[1]], w=[nm + "b"])
                TT(eng, dst[:, :, 160:192], tm[2], tm[3], ALU.add, r=[tk[2], tk[3]], w=[nm + "b"])
                CP(eng, dst[:, :, 0:128], src[:, :, 0:128], r=[nm + "f"], w=[nm + "b"])
            if stop < 4:
                continue
            var = _os.environ.get("PH1C_VAR", "")
            for h in range(4):
                if var == "notr":
                    continue
                TR(pb4[:, h * 128:(h + 1) * 128], qb[:, h, 0:128], r=["qb"], w=["pb4"])
                TR(pb4[:, 512 + h * 128:512 + (h + 1) * 128], kb[:, h, 0:128], r=["kb"], w=["pb4"])
            for h in range(4):
                if var == "nomm":
                    continue
                MM(PB[5][0:64, h * 128:(h + 1) * 128], qb[:, h, 128:192], IDB, True, True, r=["qb", "consts"], w=["pb5"])
                MM(PB[3][0:64, h * 128:(h + 1) * 128], kb[:, h, 128:192], IDB, True, True, r=["kb", "consts"], w=["pb3"])
            CP("act", QKn_s, pb4[:, 0:1024], r=["pb4"], w=["QTn_s", "KTn_s"])
            CP("dve", QTr_s[0:64, :], PB[5][0:64, :], r=["pb5"], w=["QTr_s"])
            CP("dve", KTr_s[0:64, :], PB[3][0:64, :], r=["pb3"], w=["KTr_s"])
            if stop < 5:
                continue
            for (dst_d, src_s, nm, npart) in ((QTn_d, QTn_s, "QTn", 128), (KTn_d, KTn_s, "KTn", 128),
                                             (QTr_d, QTr_s, "QTr", 64), (KTr_d, KTr_s, "KTr", 64)):
                DMA("sp", dst_d[:, :, t0:t0 + 128].rearrange("h p t -> p h t"),
                    src_s[0:npart, :].rearrange("p (h t) -> p h t", h=4), nm + "st", r=[nm + "_s"], w=[nm + "_d"])
            if i == 0:
                DBG("dbg_q", qf.rearrange("p h d -> p (h d)"), ["qf"])
                DBG("dbg_k", kf.rearrange("p h d -> p (h d)"), ["kf"])

    def phase1d():
        QB = min(512, S)
        NQB = S // QB
        KPB = QB // 128
        scale = 1.0 / math.sqrt(192.0)
        for h in range(4):
            R.barrier(skip=("precast",))
            C = Carver(NOWBC_END)
            KTn = C.b(S)
            KTr = C.b(S)
            Vh = C.b(NT * 128).rearrange("p (n d) -> p n d", n=NT)
            Qn = [C.b(QB) for _ in range(2)]
            Qr = [C.b(QB) for _ in range(2)]
            PT = [C.b(QB) for _ in range(3)]
            OT = [C.b(QB) for _ in range(2)]
            rec = C.f(QB)
            MSK = [C.b(QB) for _ in range(KPB)]
            u = f"a{h}"
            for d_ in range(KPB):
                def _mk(e, t=MSK[d_], d_=d_):
                    e.memset(t, 1.0)
                    return e.affine_select(out=t, in_=t, pattern=[[1, QB]], compare_op=ALU.is_ge, fill=0.0,
                                           base=-128 * d_, channel_multiplier=-1)
                R.add("pool", _mk, w=[u + f"msk{d_}"])
            DMA("sp", KTn, KTn_d[h], "ktn", r=["KTn_d"], w=[u + "ktn"])
            DMA("sp", KTr[0:64, :], KTr_d[h], "ktr", r=["KTr_d"], w=[u + "ktr"])
            DMA("sp", Vh, V_d[h].rearrange("(n p) d -> p n d", p=128), "vh", r=["V_d"], w=[u + "vh"])
            if h == 0 and do2 and not _os2.environ.get("SKIP_PRECAST"):
                precast("A")
            its = [(qb_, kt) for qb_ in range(NQB) for kt in range((qb_ + 1) * KPB)]
            LA = 2

            def emit_S(idx):
                qb_, kt = its[idx]
                qs_ = qb_ % 2
                q0 = qb_ * QB
                if kt == 0:
                    DMA("sp", Qn[qs_], QTn_d[h, :, q0:q0 + QB], f"qn{qs_}", r=["QTn_d"], w=[f"qn{qs_}"])
                    DMA("sp", Qr[qs_][0:64, :], QTr_d[h, :, q0:q0 + QB], f"qr{qs_}", r=["QTr_d"], w=[f"qr{qs_}"])
                bi = idx % 4
                MM(PB[bi][:, 0:QB], KTn[:, kt * 128:(kt + 1) * 128], Qn[qs_], True, False,
                   r=[u + "ktn", f"qn{qs_}"], w=[f"pb{bi}"])
                MM(PB[bi][:, 0:QB], KTr[0:64, kt * 128:(kt + 1) * 128], Qr[qs_][0:64, :], False, True,
                   r=[u + "ktr", f"qr{qs_}"], w=[f"pb{bi}"])

            def emit_rest(idx):
                qb_, kt = its[idx]
                qs_ = qb_ % 2
                q0 = qb_ * QB
                nkt = (qb_ + 1) * KPB
                ob, lbk = 4 + qs_, 6 + qs_
                bi = idx % 4
                ps_ = idx % 3
                ACT(PT[ps_], PB[bi][:, 0:QB], AF.Exp, r=[f"pb{bi}"], w=[f"pt{ps_}"], scale=scale)
                d = kt - qb_ * KPB
                if d >= 0:
                    TT("dve", PT[ps_], PT[ps_], MSK[d], ALU.mult, r=[f"pt{ps_}", u + f"msk{d}"], w=[f"pt{ps_}"])
                MM(PB[ob][:, 0:QB], Vh[:, kt, :], PT[ps_], kt == 0, kt == nkt - 1,
                   r=[u + "vh", f"pt{ps_}"], w=[f"pb{ob}"])
                MM(PB[lbk][:, 0:QB], ones_b, PT[ps_], kt == 0, kt == nkt - 1,
                   r=["consts", f"pt{ps_}"], w=[f"pb{lbk}"])
                if kt == nkt - 1:
                    R.add("dve", lambda e, o=rec, i_=PB[lbk][:, 0:QB]: e.reciprocal(out=o, in_=i_),
                          r=[f"pb{lbk}"], w=["rec"])
                    TT("dve", OT[qs_], PB[ob][:, 0:QB], rec, ALU.mult, r=[f"pb{ob}", "rec"], w=[f"ot{qs_}"])
                    OSB = min(QB, S4)
                    for part in range(QB // OSB):
                        T0 = q0 + part * OSB
                        dq, tq0 = T0 // S4, T0 % S4
                        DMA("sp", a2a_in[dq, 512 + h * 128:512 + (h + 1) * 128, tq0:tq0 + OSB],
                            OT[qs_][:, part * OSB:(part + 1) * OSB], f"ot{qs_}", r=[f"ot{qs_}"], w=["a2a_in"])

            for idx in range(len(its) + LA):
                if idx < len(its):
                    emit_S(idx)
                if idx - LA >= 0:
                    emit_rest(idx - LA)

    if do1:
        x_seq = din("x_seq", [S, D])
        pos = din("pos", [1, S], I32)
        w_hg = din("w_hg", [D, 2048])
        w_ml = din("w_ml", [D, 1344])
        n1w = din("n1w", [1, D])
        lb_raw = din("lb_raw", [2, 512])
        hgnw = din("hgnw", [1, 128])
        qnw = din("qnw", [1, 768])
        kvnw = din("kvnw", [1, 512])
        qhw = din("qhw", [1, 192])
        khw = din("khw", [1, 192])
        w_uq = din("w_uq", [768, 768])
        w_ukv = din("w_ukv", [512, 1024])
        invf = din("invf", [1, 32])
        xnT_d = dscr("xnT_d", [NT, 128, KT * 128], BF16)
        QTn_d = dscr("QTn_d", [4, 128, S], BF16)
        QTr_d = dscr("QTr_d", [4, 64, S], BF16)
        KTn_d = dscr("KTn_d", [4, 128, S], BF16)
        KTr_d = dscr("KTr_d", [4, 64, S], BF16)
        V_d = dscr("V_d", [4, S, 128], BF16)
    if mode == "p1":
        a2a_in = dout("a2a_in", [4, 1024, S4], BF16)
    elif mode == "fused":
        a2a_in = dscr("a2a_in", [4, 1024, S4], BF16)
    if mode == "p2":
        mix_all = din("mix_all", [4, 1024, S4], BF16)
    elif mode == "fused":
        CR = min(1024, (1 << 20) // (S4 * 2))
        NQ = 1024 // CR
        gathered = dscr("gathered", [4, NQ, 4, CR, S4], BF16)
        sel = din("sel", [1, 4])
    if do2:
        x_res = din("x_res", [S4, D])
        n2w = din("n2w", [1, D])
        w_o = din("w_o", [4096, D])
        w_up = din("w_up", [D, DFF])
        w_dn = din("w_dn", [DFF, D])
        out = dout("out", [S4, D], F32)
        hbuf = dscr("hbuf", [S4, D], F32)
        CW = min(512, D)
        NCW = D // CW
        SUB = min(16, FH)
        wo_t = dscr("wo_t", [NCW, 128, 32 * CW], BF16)
        wup_t = dscr("wup_t", [DFF // 256, 128, KT * 256], BF16)
        wdn_t = dscr("wdn_t", [NCW, DFF // (128 * SUB), 128, SUB * CW], BF16)

    if do2 and not do1 and not _os2.environ.get("SKIP_PRECAST"):
        precast("A")
        precast("B")
    if do1:
        import os as _os
        upto = _os.environ.get("PH1_UPTO", "d")
        phase1a()
        if upto >= "b":
            phase1b()
        if upto >= "c":
            phase1c()
        if upto >= "d":
            phase1d()
    if mode == "fused":
        R.barrier(skip=("precast",))
        def AG(in_ap, out_ap):
            R.add("pool", lambda e: e.collective_compute(
                "AllGather", ALU.bypass, replica_groups=[[0, 1, 2, 3], [4, 5, 6, 7]],
                ins=[in_ap], outs=[out_ap]), r=["a2a_in"], w=["mix_all"], dma="ag", inc=1)
        for d in range(4 if not _os2.environ.get("SKIP_AG") else 0):
            for q in range(NQ):
                AG(a2a_in[d, q * CR:(q + 1) * CR, :].opt(),
                   gathered[d, q].rearrange("s r t -> (s r) t").opt())
    if do2 and do1 and not _os2.environ.get("SKIP_PRECAST"):
        precast("B")
    if do2:
        phase2()

    R.barrier()
    R.add("sp", lambda e: e.nop(), r=[], w=[])

    sems = {}
    for en in Rec.ENGS:
        s = nc.semaphore("s_" + en)
        ctx.append(s)
        sems[en] = s.__enter__()
    dsems = {}
    for i, k in enumerate(R.dma_keys()):
        s = nc.semaphore(f"d{i}")
        ctx.append(s)
        dsems[k] = s.__enter__()
    with nc.Block() as block:
        R.emit(block, sems, dsems)
    return nc


MODE = "fused"
_NC_CACHE = {}


def _get_nc(cfg, mode):
    key = (cfg.D, cfg.S, cfg.DFF, mode)
    if key not in _NC_CACHE:
        _NC_CACHE[key] = build(cfg, mode)
    return _NC_CACHE[key]


def shard_inputs(inp):
    x = np.asarray(inp["x"])
    B, S, D = x.shape
    w_in = np.asarray(inp["w_in"])[0]
    splits = [2048, 2048, 2048, 2048, 768, 512, 64]
    offs = np.cumsum([0] + splits)
    q_, f_, i_, g_, cq_, ckv_, kr_ = [w_in[:, offs[k]:offs[k + 1]] for k in range(7)]
    w_ml = np.ascontiguousarray(np.concatenate([cq_, kr_, ckv_], axis=1))
    w_o = np.asarray(inp["w_o"])[0]
    w_o_p = np.ascontiguousarray(np.concatenate(
        [np.concatenate([w_o[s * 512:(s + 1) * 512], w_o[2048 + s * 512:2048 + (s + 1) * 512]], axis=0)
         for s in range(4)], axis=0))
    w_up = np.ascontiguousarray(np.asarray(inp["w_up"])[0])
    w_dn = np.ascontiguousarray(np.asarray(inp["w_down"])[0])
    invf = (np.float32(10000.0) ** (-np.arange(0, 64, 2, dtype=np.float32) / np.float32(64))).astype(np.float32)[None, :]
    pos = np.asarray(inp["positions"]).astype(np.int32)
    S4 = S // 4
    maps1, maps2 = [], []
    for c in range(8):
        b, j = c // 4, c % 4
        hs = slice(j * 512, (j + 1) * 512)
        wukv = np.asarray(inp["w_ukv"])[0].reshape(512, 16, 256)[:, 4 * j:4 * j + 4, :]
        m1 = {
            "x_seq": np.ascontiguousarray(x[b]),
            "pos": np.ascontiguousarray(pos[b:b + 1]),
            "w_hg": np.ascontiguousarray(np.concatenate([q_[:, hs], f_[:, hs], i_[:, hs], g_[:, hs]], axis=1)),
            "w_ml": w_ml,
            "n1w": np.ascontiguousarray(np.asarray(inp["norm1_w"])[0:1]),
            "lb_raw": np.ascontiguousarray(np.asarray(inp["hgrn_lb"])[:, hs]),
            "hgnw": np.ascontiguousarray(np.asarray(inp["hgrn_out_norm_w"])[0:1]),
            "qnw": np.ascontiguousarray(np.asarray(inp["mla_q_norm_w"])[0:1]),
            "kvnw": np.ascontiguousarray(np.asarray(inp["mla_kv_norm_w"])[0:1]),
            "qhw": np.ascontiguousarray(np.asarray(inp["q_head_norm_w"])[0:1]),
            "khw": np.ascontiguousarray(np.asarray(inp["k_head_norm_w"])[0:1]),
            "w_uq": np.ascontiguousarray(np.asarray(inp["w_uq"])[0][:, j * 768:(j + 1) * 768]),
            "w_ukv": np.ascontiguousarray(np.concatenate(
                [wukv[:, :, :128].reshape(512, 512), wukv[:, :, 128:].reshape(512, 512)], axis=1)),
            "invf": invf,
        }
        selv = np.zeros((1, 4), np.float32)
        selv[0, j] = 1.0
        m2 = {
            "sel": selv,
            "x_res": np.ascontiguousarray(x[b, j * S4:(j + 1) * S4]),
            "n2w": np.ascontiguousarray(np.asarray(inp["norm2_w"])[0:1]),
            "w_o": w_o_p,
            "w_up": w_up,
            "w_dn": w_dn,
        }
        maps1.append(m1)
        maps2.append(m2)
    return maps1, maps2


def kernel(**inputs):
    x = np.asarray(inputs["x"])
    B, S, D = x.shape
    DFF = np.asarray(inputs["w_up"]).shape[-1]
    cfg = Cfg(D=D, S=S, DFF=DFF)
    S4 = S // 4
    maps1, maps2 = shard_inputs(inputs)
    cores = list(range(8))
    if MODE == "fused":
        nc = _get_nc(cfg, "fused")
        res = run_bass_kernel_spmd(nc, [dict(a, **b_) for a, b_ in zip(maps1, maps2)], core_ids=cores)
        outs = [r["out"] for r in res.results]
    else:
        nc1 = _get_nc(cfg, "p1")
        r1 = run_bass_kernel_spmd(nc1, maps1, core_ids=cores).results
        for c in range(8):
            b, j = c // 4, c % 4
            maps2[c].pop("sel", None)
            maps2[c]["mix_all"] = np.ascontiguousarray(
                np.stack([np.asarray(r1[b * 4 + s]["a2a_in"])[j] for s in range(4)], axis=0))
        nc2 = _get_nc(cfg, "p2")
        r2 = run_bass_kernel_spmd(nc2, maps2, core_ids=cores).results
        outs = [r["out"] for r in r2]
    full = np.empty((B, S, D), np.float32)
    for c in range(8):
        b, j = c // 4, c % 4
        full[b, j * S4:(j + 1) * S4] = np.asarray(outs[c], dtype=np.float32)
    return full
```

```python
import math
import numpy as np
import concourse.bass as bass
import concourse.mybir as mybir
from concourse.bass_utils import run_bass_kernel_spmd

F32 = mybir.dt.float32
BF16 = mybir.dt.bfloat16
I32 = mybir.dt.int32
AF = mybir.ActivationFunctionType
ALU = mybir.AluOpType
EPS = 1e-6
ARENA_ELEMS = 106400


class Rec:
    ENGS = ("pe", "act", "dve", "pool", "sp")

    def __init__(self):
        self.ops = []
        self.lw = {}
        self.rd = {}
        self.barrier_deps = {e: set() for e in self.ENGS}

    def add(self, eng, fn, r=(), w=(), dma=None, inc=16):
        i = len(self.ops)
        deps = {}
        for t in r:
            if t in self.lw:
                deps[self.lw[t]] = "raw"
        for t in w:
            for q in self.rd.get(t, ()):
                deps.setdefault(q, "war")
            if t in self.lw:
                deps.setdefault(self.lw[t], "waw")
        for d in self.barrier_deps[eng]:
            deps[d] = "raw"
        self.barrier_deps[eng] = set()
        for t in r:
            self.rd.setdefault(t, []).append(i)
        for t in w:
            self.lw[t] = i
            self.rd[t] = []
        self.ops.append(dict(eng=eng, fn=fn, deps=deps, dma=dma, inc=inc))
        return i

    def barrier(self, skip=()):
        last = {}
        for i, o in enumerate(self.ops):
            if o["dma"] is not None:
                if o["dma"] in skip:
                    continue
                last[("dma", o["dma"])] = i
            else:
                last[o["eng"]] = i
        s = set(last.values())
        for e in self.ENGS:
            self.barrier_deps[e] = set(s)

    def emit(self, block, sems, dma_sems):
        ops = self.ops
        need = set()
        for o in ops:
            for d, kind in o["deps"].items():
                po = ops[d]
                if po["dma"] is not None:
                    continue
                if po["eng"] == o["eng"] and kind != "raw":
                    continue
                need.add(d)
        cnt = {e: 0 for e in self.ENGS}
        dcnt = {}
        for i, o in enumerate(ops):
            if o["dma"] is not None:
                k = o["dma"]
                dcnt[k] = dcnt.get(k, 0) + o["inc"]
                o["sig"] = ("d", k, dcnt[k])
            elif i in need:
                cnt[o["eng"]] += 1
                o["sig"] = ("c", o["eng"], cnt[o["eng"]])
            else:
                o["sig"] = None
        per = {e: [] for e in self.ENGS}
        for i, o in enumerate(ops):
            per[o["eng"]].append(i)

        def run(engname, eng):
            waited = {}
            for i in per[engname]:
                o = ops[i]
                wl = {}
                for d, kind in o["deps"].items():
                    po = ops[d]
                    if po["dma"] is None and po["eng"] == engname and kind != "raw":
                        continue
                    s = po["sig"]
                    if s is None:
                        continue
                    key = (s[0], s[1])
                    wl[key] = max(wl.get(key, 0), s[2])
                for key, val in wl.items():
                    if waited.get(key, 0) >= val:
                        continue
                    waited[key] = val
                    sem = dma_sems[key[1]] if key[0] == "d" else sems[key[1]]
                    eng.wait_ge(sem, val)
                ins = o["fn"](eng)
                s = o["sig"]
                if s is not None:
                    if s[0] == "d":
                        ins.then_inc(dma_sems[s[1]], o["inc"])
                    else:
                        ins.then_inc(sems[s[1]], 1)

        block.tensor(lambda e: run("pe", e))
        block.scalar(lambda e: run("act", e))
        block.vector(lambda e: run("dve", e))
        block.gpsimd(lambda e: run("pool", e))
        block.sync(lambda e: run("sp", e))

    def dma_keys(self):
        ks = []
        for o in self.ops:
            if o["dma"] is not None and o["dma"] not in ks:
                ks.append(o["dma"])
        return ks


class Cfg:
    def __init__(self, D=4096, S=8192, DFF=16384):
        self.D = D
        self.S = S
        self.DFF = DFF
        self.KT = D // 128
        self.NT = S // 128
        self.S4 = S // 4
        self.TB = min(512, self.S4)
        self.NB = self.S4 // self.TB
        self.TPB = self.TB // 128
        nff = DFF // 128
        self.FH = 64 if nff >= 128 else nff // 2
        self.NH = nff // self.FH


def build(cfg, mode="fused", debug=None):
    D, S, DFF, KT, NT, S4 = cfg.D, cfg.S, cfg.DFF, cfg.KT, cfg.NT, cfg.S4
    TB, NB, TPB, FH, NH = cfg.TB, cfg.NB, cfg.TPB, cfg.FH, cfg.NH
    import os as _os2
    do1 = mode in ("fused", "p1")
    do2 = mode in ("fused", "p2")
    nc = bass.Bass("TRN2", target_bir_lowering=False)
    R = Rec()

    def din(name, shape, dt=F32):
        return nc.dram_tensor(name, list(shape), dt, kind="ExternalInput").ap()

    def dscr(name, shape, dt):
        return nc.dram_tensor(name, list(shape), dt).ap()

    def dout(name, shape, dt):
        return nc.dram_tensor(name, list(shape), dt, kind="ExternalOutput").ap()

    dbg = {}
    if debug:
        for nm, shp in debug.items():
            dbg[nm] = dout(nm, shp, F32)

    ctx = []
    t = nc.sbuf_tensor("arena", [128, ARENA_ELEMS], BF16)
    ctx.append(t)
    ARENA = t.__enter__()
    PB = []
    for i in range(8):
        t = nc.psum_tensor(f"pb{i}", [128, 512], F32)
        ctx.append(t)
        PB.append(t.__enter__())

    class Carver:
        def __init__(self, start=0):
            self.off = start

        def b(self, n):
            o = self.off
            self.off += n + (n & 1)
            assert self.off <= ARENA_ELEMS, self.off
            return ARENA[:, o:o + n]

        def f(self, n, dt=F32):
            o = self.off
            self.off += 2 * n
            assert self.off <= ARENA_ELEMS, self.off
            return ARENA[:, o:o + 2 * n].bitcast(dt)

    CV = Carver(0)
    IDB = CV.b(128)
    ones_b = CV.b(128)
    tri_in = CV.f(128)
    tri_af = CV.f(128)
    ind2 = CV.f(2)
    pi_t = CV.f(2)[:, 0:1]
    npi_t = CV.f(2)[:, 0:1]
    SEL = CV.f(4)
    NOWBC_END = CV.off
    WBC = CV.f(D)
    PERSIST_END = CV.off

    def _consts(e):
        e.memset(IDB, 0.0)
        e.affine_select(out=IDB, in_=IDB, pattern=[[-1, 128]], compare_op=ALU.not_equal,
                        fill=1.0, base=0, channel_multiplier=1)
        e.memset(tri_in, 1.0)
        e.affine_select(out=tri_in, in_=tri_in, pattern=[[1, 128]], compare_op=ALU.is_ge,
                        fill=0.0, base=0, channel_multiplier=-1)
        e.memset(tri_in[0:64, 64:128], 0.0)
        e.memset(tri_af, 1.0)
        e.affine_select(out=tri_af, in_=tri_af, pattern=[[-1, 128]], compare_op=ALU.is_gt,
                        fill=0.0, base=0, channel_multiplier=1)
        e.memset(tri_af[64:128, 0:64], 0.0)
        e.memset(ind2, 0.0)
        e.memset(ind2[0:64, 0:1], 1.0)
        e.memset(ind2[64:128, 1:2], 1.0)
        e.memset(pi_t, math.pi)
        e.memset(npi_t, -math.pi)
        return e.memset(ones_b, 1.0)

    R.add("pool", _consts, w=["consts"])

    def DMA(q, out_ap, in_ap, key, r=(), w=()):
        return R.add(q, lambda e: e.dma_start(out=out_ap, in_=in_ap), r=r, w=w, dma=key)

    def MM(out_ap, lhsT, rhs, start, stop, r=(), w=(), tp=None):
        if tp is None:
            return R.add("pe", lambda e: e.matmul(out_ap, lhsT=lhsT, rhs=rhs, start=start, stop=stop), r=r, w=w)
        return R.add("pe", lambda e: e.matmul(out_ap, lhsT=lhsT, rhs=rhs, start=start, stop=stop,
                                              tile_position=tp), r=r, w=w)

    def TR(out_ap, in_ap, r=(), w=()):
        n = in_ap.shape[0]
        return R.add("pe", lambda e: e.transpose(out_ap, in_ap, IDB[0:n, 0:n]), r=list(r) + ["consts"], w=w)

    def ACT(out_ap, in_ap, func, r=(), w=(), bias=None, scale=None, accum=None):
        kw = {}
        if bias is not None:
            kw["bias"] = bias
        if scale is not None:
            kw["scale"] = scale
        if accum is not None:
            kw["accum_out"] = accum
        return R.add("act", lambda e: e.activation(out=out_ap, in_=in_ap, func=func, **kw), r=r, w=w)

    def TS(eng, out_ap, in0, s1, s2, op0, op1=None, r=(), w=()):
        if op1 is None:
            return R.add(eng, lambda e: e.tensor_scalar(out=out_ap, in0=in0, scalar1=s1, scalar2=None, op0=op0), r=r, w=w)
        return R.add(eng, lambda e: e.tensor_scalar(out=out_ap, in0=in0, scalar1=s1, scalar2=s2, op0=op0, op1=op1), r=r, w=w)

    def STT(eng, out_ap, in0, scalar, in1, op0, op1, r=(), w=()):
        return R.add(eng, lambda e: e.scalar_tensor_tensor(out=out_ap, in0=in0, scalar=scalar, in1=in1,
                                                           op0=op0, op1=op1), r=r, w=w)

    def TT(eng, out_ap, in0, in1, op, r=(), w=()):
        return R.add(eng, lambda e: e.tensor_tensor(out=out_ap, in0=in0, in1=in1, op=op), r=r, w=w)

    def CP(eng, out_ap, in_ap, r=(), w=()):
        if eng == "act":
            return R.add("act", lambda e: e.activation(out=out_ap, in_=in_ap, func=AF.Copy), r=r, w=w)
        return R.add(eng, lambda e: e.tensor_copy(out=out_ap, in_=in_ap), r=r, w=w)

    def rstd_from_ss(rs, ss, n, r, w):
        TS("dve", rs, ss, 1.0 / n, EPS, ALU.mult, ALU.add, r=r, w=w)
        ACT(rs, rs, AF.Sqrt, r=w, w=w)
        R.add("dve", lambda e: e.reciprocal(out=rs, in_=rs), r=w, w=w)

    def DBG(name, src_ap, r):
        if name in dbg:
            DMA("sp", dbg[name], src_ap, "dbg_" + name, r=r, w=["dbg_" + name])

    def norm_transpose(src_rows, slot, XT, XN, SSR, wtok, dst_fn, dst_tok, rd_tok=(), uid=""):
        xt = XT[slot]
        xtk = f"xt{slot}"
        DMA("sp", xt, src_rows, xtk, r=list(rd_tok), w=[xtk])
        ss = SSR[:, slot:slot + 1]
        rs = SSR[:, 2 + slot:3 + slot]
        XNs = XN[slot]
        xnk = f"xn{slot}"
        ACT(XN[2], xt, AF.Square, r=[xtk], w=["xnjunk", f"ss{slot}"], accum=ss)
        rstd_from_ss(rs, ss, D, r=[f"ss{slot}"], w=[f"rs{slot}"])
        STT("dve", XNs, xt, rs, WBC, ALU.mult, ALU.mult,
            r=[xtk, f"rs{slot}", wtok], w=[xnk])
        ngrp = (KT + 7) // 8
        for g in range(ngrp):
            bi = 4 + (g % 4)
            pbf = PB[bi][:].bitcast(BF16)
            nk = min(8, KT - g * 8)
            for kk in range(nk):
                k = g * 8 + kk
                TR(pbf[:, kk * 128:(kk + 1) * 128], XNs[:, k * 128:(k + 1) * 128],
                   r=[xnk], w=[f"pb{bi}"])
            src = pbf[:, 0:nk * 128].rearrange("p (k t) -> p k t", k=nk)
            CP("act" if g % 2 == 0 else "dve", dst_fn(g, nk), src, r=[f"pb{bi}"], w=[dst_tok])

    def precast(part):
        if part == "B":
            for g in range(DFF // (128 * SUB)):
                for c in range(NCW):
                    DMA("pool", wdn_t[c, g].rearrange("p (j n) -> p j n", j=SUB),
                        w_dn[g * SUB * 128:(g + 1) * SUB * 128, c * CW:(c + 1) * CW].rearrange("(j p) n -> p j n", p=128),
                        "precast", w=["wdn_b"])
            return
        for c in range(NCW):
            DMA("pool", wo_t[c].rearrange("p (k n) -> p k n", k=32),
                w_o[:, c * CW:(c + 1) * CW].rearrange("(k p) n -> p k n", p=128), "precast", w=["wo_b"])
        rows = max(1, min(D, (8 << 20) // (DFF * 4)))
        for i in range(D // rows):
            DMA("pool", wup_b[i * rows:(i + 1) * rows, :], w_up[i * rows:(i + 1) * rows, :], "precast", w=["wup_b"])

    def phase2():
        nonlocal_uid = [0]
        DMA("sp", WBC, n2w[0:1, :].broadcast_to([128, D]), "wbc", w=["wbc2"])
        if mode == "fused":
            DMA("sp", SEL, sel[0:1, :].broadcast_to([128, 4]), "sel", w=["sel"])
        else:
            mixv = mix_all.rearrange("s (r p) t -> p (s r) t", p=128)
        NC = NCW
        for tb in range(NB):
            t0 = tb * TB
            R.barrier(skip=("precast",))
            C = Carver(PERSIST_END)
            MIXT = C.b(32 * TB).rearrange("p (k t) -> p k t", k=32)
            WO = [C.b(32 * CW).rearrange("p (k n) -> p k n", k=32) for _ in range(2)]
            RIN = [C.f(TPB * CW).rearrange("p (i n) -> p i n", i=TPB) for _ in range(2)]
            ROUT = [C.f(TPB * CW).rearrange("p (i n) -> p i n", i=TPB) for _ in range(2)]
            u = f"b{tb}"
            if mode == "fused":
                STG = [C.b(8 * TB).rearrange("p (r t) -> p r t", r=8) for _ in range(2)]
                si = 0
                for s_ in range(4):
                    dstv = MIXT[:, s_ * 8:(s_ + 1) * 8, :]
                    for d in range(4):
                        sg = si % 2
                        si += 1
                        rpq = CR // 128
                        for q in range(NQ):
                            DMA("sp", STG[sg][:, q * rpq:(q + 1) * rpq, :],
                                gathered[d, q, s_, :, t0:t0 + TB].rearrange("(r p) t -> p r t", p=128),
                                f"stg{sg}", r=["mix_all"], w=[f"stg{sg}"])
                        if d == 0:
                            TS("dve", dstv, STG[sg], SEL[:, 0:1], None, ALU.mult,
                               r=[f"stg{sg}", "sel"], w=[u + "mixt"])
                        else:
                            STT("dve", dstv, STG[sg], SEL[:, d:d + 1], dstv, ALU.mult, ALU.add,
                                r=[f"stg{sg}", "sel", u + "mixt"], w=[u + "mixt"])
            else:
                DMA("sp", MIXT, mixv[:, :, t0:t0 + TB], "mixt", r=["mix_all"], w=[u + "mixt"])
            for c in range(NC):
                sl = c % 2
                DMA("sp", WO[sl], wo_t[c].rearrange("p (k n) -> p k n", k=32), f"wo{sl}", r=["wo_b"], w=[f"wo{sl}"])
                DMA("sp", RIN[sl], x_res[t0:t0 + TB, c * CW:(c + 1) * CW].rearrange("(i p) n -> p i n", p=128),
                    f"rin{sl}", w=[f"rin{sl}"])
                for ti in range(TPB):
                    bi = (c % 2) * 4 + ti
                    for k in range(32):
                        MM(PB[bi][:, 0:CW], MIXT[:, k, ti * 128:(ti + 1) * 128], WO[sl][:, k, :],
                           k == 0, k == 31, r=[u + "mixt", f"wo{sl}"], w=[f"pb{bi}"])
                    TT("dve", ROUT[sl][:, ti, :], PB[bi][:, 0:CW], RIN[sl][:, ti, :], ALU.add,
                       r=[f"pb{bi}", f"rin{sl}"], w=[f"rout{sl}"])
                DMA("sp", hbuf[t0:t0 + TB, c * CW:(c + 1) * CW].rearrange("(i p) n -> p i n", p=128),
                    ROUT[sl], f"rout{sl}", r=[f"rout{sl}"], w=[f"hbuf{tb}_{c}"])
            R.barrier(skip=("precast",))
            C = Carver(PERSIST_END)
            N2T = C.b(KT * TB).rearrange("p (k t) -> p k t", k=KT)
            HID0 = C.off
            XT = [C.f(D) for _ in range(2)]
            XN = [C.b(D) for _ in range(3)]
            SSR = C.f(4)
            for ti in range(TPB):
                norm_transpose(hbuf[t0 + ti * 128:t0 + (ti + 1) * 128, :], ti % 2, XT, XN, SSR, "wbc2",
                               lambda g, nk, ti=ti: N2T[:, g * 8:g * 8 + nk, ti * 128:(ti + 1) * 128],
                               u + "n2t", rd_tok=[f"hbuf{tb}_{c}" for c in range(NC)])
            R.barrier(skip=("precast",))
            C = Carver(HID0)
            HIDT = C.b(FH * TB).rearrange("p (j t) -> p j t", j=FH)
            NWB = 3
            WB = [C.b(4096 * 2) for _ in range(NWB)]
            RIN = [C.f(TPB * CW).rearrange("p (i n) -> p i n", i=TPB) for _ in range(2)]
            ROUT = [C.f(TPB * CW).rearrange("p (i n) -> p i n", i=TPB) for _ in range(2)]
            RELU = [C.f(TB) for _ in range(2)]
            wi = 0
            for hf in range(NH):
                uh = f"{u}h{hf}"
                for sidx in range(FH // 2):
                    col0 = (hf * FH + sidx * 2) * 128
                    sl = wi % NWB
                    wi += 1
                    wsl = WB[sl][:, 0:KT * 256].rearrange("p (k n) -> p k n", k=KT)
                    DMA("sp", wsl, wupv[:, :, col0:col0 + 256], f"wb{sl}", r=["wup_b"], w=[f"wb{sl}"])
                    for j in range(2):
                        fc = sidx * 2 + j
                        bi = fc % 4
                        for k in range(KT):
                            MM(PB[bi][:, 0:TB], wsl[:, k, j * 128:(j + 1) * 128], N2T[:, k, :],
                               k == 0, k == KT - 1, r=[f"wb{sl}", u + "n2t"], w=[f"pb{bi}"])
                        rl = RELU[fc % 2]
                        ACT(rl, PB[bi][:, 0:TB], AF.Relu, r=[f"pb{bi}"], w=[f"relu{fc % 2}"])
                        TT("dve", HIDT[:, fc, :], rl, rl, ALU.mult,
                           r=[f"relu{fc % 2}"], w=[f"{uh}hid{fc}"])
                last = hf == NH - 1
                for c in range(NC):
                    sl2 = c % 2
                    for sub in range(FH // SUB):
                        sl = wi % NWB
                        wi += 1
                        wsl = WB[sl][:, 0:SUB * CW].rearrange("p (j n) -> p j n", j=SUB)
                        DMA("sp", WB[sl][:, 0:SUB * CW], wdn_t[c, (hf * FH) // SUB + sub],
                            f"wb{sl}", r=["wdn_b"], w=[f"wb{sl}"])
                        for j in range(SUB):
                            fc = sub * SUB + j
                            for ti in range(TPB):
                                bi = (c % 2) * 4 + ti
                                MM(PB[bi][:, 0:CW], HIDT[:, fc, ti * 128:(ti + 1) * 128], wsl[:, j, :],
                                   fc == 0, fc == FH - 1, r=[f"wb{sl}", f"{uh}hid{fc}"], w=[f"pb{bi}"])
                    DMA("sp", RIN[sl2], hbuf[t0:t0 + TB, c * CW:(c + 1) * CW].rearrange("(i p) n -> p i n", p=128),
                        f"rin{sl2}", r=[f"hbuf{tb}_{c}"], w=[f"rin{sl2}"])
                    for ti in range(TPB):
                        bi = (c % 2) * 4 + ti
                        TT("dve", ROUT[sl2][:, ti, :], PB[bi][:, 0:CW], RIN[sl2][:, ti, :], ALU.add,
                           r=[f"pb{bi}", f"rin{sl2}"], w=[f"rout{sl2}"])
                    dst = out if last else hbuf
                    DMA("sp", dst[t0:t0 + TB, c * CW:(c + 1) * CW].rearrange("(i p) n -> p i n", p=128),
                        ROUT[sl2], f"rout{sl2}", r=[f"rout{sl2}"], w=[f"hbuf{tb}_{c}"])


    def phase1a():
        R.barrier(skip=("precast",))
        DMA("sp", WBC, n1w[0:1, :].broadcast_to([128, D]), "wbc", w=["wbc1"])
        C = Carver(PERSIST_END)
        XT = [C.f(D) for _ in range(2)]
        XN = [C.b(D) for _ in range(3)]
        SSR = C.f(4)
        XNT = [C.b(KT * 128) for _ in range(2)]
        for i in range(NT):
            sl = i % 2
            v3 = XNT[sl].rearrange("p (k t) -> p k t", k=KT)
            norm_transpose(x_seq[i * 128:(i + 1) * 128, :], sl, XT, XN, SSR, "wbc1",
                           lambda g, nk, v3=v3: v3[:, g * 8:g * 8 + nk, :], f"xnt{sl}")
            DMA("sp", xnT_d[i], XNT[sl], f"xnts{sl}", r=[f"xnt{sl}"], w=[f"xnTd{i}"])

    def phase1b():
        R.barrier(skip=("precast",))
        C = Carver(NOWBC_END)
        WHG = C.b(KT * 2048).rearrange("p (k n) -> p k n", k=KT)
        XNT = [C.b(KT * 128) for _ in range(2)]
        sgn, kk, logf, eb, enb, er, qs, gate, on = [C.f(512) for _ in range(9)]
        q_dec, k_inv, k_tail, v_b, og, AT_b, state_b, junk = [C.b(512) for _ in range(8)]
        QKT = C.b(1024)
        q_decT, k_invT = QKT[:, 0:512], QKT[:, 512:1024]
        state_m = C.b(512)
        state_f = C.f(512)
        oml = C.f(512)
        lb1 = C.f(512)
        atmask = C.f(512)
        hgw = C.f(128)
        decT = C.f(8)
        oss = C.f(4)
        rso = C.f(4)
        one_t = C.f(2)[:, 0:1]
        OSB = min(512, S4)
        OSTN = OSB // 128
        OST = [C.b(4 * OSB).rearrange("p (h t) -> p h t", h=4) for _ in range(2)]
        kc = max(1, KT // 8)
        for g in range(KT // kc):
            DMA("pool", WHG[:, g * kc:(g + 1) * kc, :],
                w_hg[g * kc * 128:(g + 1) * kc * 128, :].rearrange("(k p) n -> p k n", p=128),
                "whg", w=["whg"])
        DMA("sp", oml, lb_raw[0:1, :].broadcast_to([128, 512]), "lb0", w=["oml"])
        DMA("sp", lb1, lb_raw[1:2, :].broadcast_to([128, 512]), "lb1", w=["lb1"])
        DMA("sp", hgw, hgnw[0:1, :].broadcast_to([128, 128]), "hgw", w=["hgw"])
        TT("dve", oml, lb1, oml, ALU.subtract, r=["lb1", "oml"], w=["oml"])
        ACT(oml, oml, AF.Sigmoid, r=["oml"], w=["oml"])
        for h in range(4):
            CP("pool", atmask[:, h * 128:(h + 1) * 128], tri_in, r=["consts"], w=["atmask"])
        R.add("pool", lambda e: e.memset(state_f, 0.0), w=["state_f"])
        R.add("pool", lambda e: e.memset(state_b, 0.0), w=["state_b"])
        R.add("pool", lambda e: e.memset(one_t, 1.0), w=["one_t"])
        pb6 = PB[6][:].bitcast(BF16)
        for i in range(NT):
            sl = i % 2
            xk = f"hxnt{sl}"
            DMA("sp", XNT[sl], xnT_d[i], xk, r=[f"xnTd{i}"], w=[xk])
            for cg in range(4):
                for k in range(KT):
                    MM(PB[cg][:], XNT[sl][:, k * 128:(k + 1) * 128], WHG[:, k, cg * 512:(cg + 1) * 512],
                       k == 0, k == KT - 1, r=[xk, "whg"], w=[f"pb{cg}"])
            ACT(sgn, PB[1][:], AF.Sigmoid, r=["pb1"], w=["sgn"], scale=-1.0)
            ACT(qs, PB[0][:], AF.Silu, r=["pb0"], w=["qs"])
            TT("dve", kk, sgn, oml, ALU.mult, r=["sgn", "oml"], w=["kk"])
            ACT(logf, kk, AF.Ln, r=["kk", "one_t"], w=["logf"], scale=-1.0, bias=one_t)
            MM(PB[4][:], tri_in, logf, True, True, r=["logf", "consts"], w=["pb4"])
            MM(PB[5][:], tri_af, logf, True, True, r=["logf", "consts"], w=["pb5"])
            for h in range(4):
                MM(PB[0][:, 2 * h:2 * h + 2], logf[:, h * 128:(h + 1) * 128], ind2, True, True,
                   r=["logf", "consts"], w=["pb0"])
            ACT(eb, PB[4][:], AF.Exp, r=["pb4"], w=["eb"])
            ACT(enb, PB[4][:], AF.Exp, r=["pb4"], w=["enb"], scale=-1.0)
            ACT(er, PB[5][:], AF.Exp, r=["pb5"], w=["er"])
            ACT(decT, PB[0][:, 0:8], AF.Exp, r=["pb0"], w=["decT"])
            ACT(gate, PB[3][:], AF.Silu, r=["pb3"], w=["gate"])
            CP("dve", v_b, PB[2][:], r=["pb2"], w=["v_b"])
            TT("dve", q_dec, qs, eb, ALU.mult, r=["qs", "eb"], w=["q_dec"])
            TT("dve", k_inv, kk, enb, ALU.mult, r=["kk", "enb"], w=["k_inv"])
            TT("pool", k_tail, kk, er, ALU.mult, r=["kk", "er"], w=["k_tail"])
            for h in range(4):
                TR(pb6[:, h * 128:(h + 1) * 128], q_dec[:, h * 128:(h + 1) * 128], r=["q_dec"], w=["pb6"])
            for h in range(4):
                TR(pb6[:, 512 + h * 128:512 + (h + 1) * 128], k_inv[:, h * 128:(h + 1) * 128], r=["k_inv"], w=["pb6"])
            CP("act" if i % 2 == 0 else "dve", QKT, pb6[:, 0:1024], r=["pb6"], w=["q_decT", "k_invT"])
            for h in range(4):
                hc = slice(h * 128, (h + 1) * 128)
                MM(PB[4][:, hc], k_invT[:, hc], q_decT[:, hc], True, True, r=["k_invT", "q_decT"], w=["pb4"])
            TT("dve", AT_b, PB[4][:], atmask, ALU.mult, r=["pb4", "atmask"], w=["AT_b"])
            def state_update(ch, dst_b, dst_tok):
                ps_ = slice(ch * 64, (ch + 1) * 64)
                for h in range(4):
                    hc = slice(h * 128, (h + 1) * 128)
                    MM(PB[7][:, hc], k_tail[ps_, hc], v_b[ps_, hc], True, True, r=["k_tail", "v_b"], w=["pb7"])
                for h in range(4):
                    hc = slice(h * 128, (h + 1) * 128)
                    STT("dve", state_f[:, hc], state_f[:, hc], decT[:, 2 * h + ch:2 * h + ch + 1], PB[7][:, hc],
                        ALU.mult, ALU.add, r=["state_f", "decT", "pb7"], w=["state_f"])
                CP("pool", dst_b, state_f, r=["state_f"], w=[dst_tok])

            state_update(0, state_m, "state_m")
            for h in range(4):
                hc = slice(h * 128, (h + 1) * 128)
                MM(PB[5][:, hc], AT_b[:, hc], v_b[:, hc], True, False, r=["AT_b", "v_b"], w=["pb5"])
                MM(PB[5][0:64, hc], q_decT[:, h * 128:h * 128 + 64], state_b[:, hc], False, False,
                   r=["q_decT", "state_b"], w=["pb5"])
                MM(PB[5][64:128, hc], q_decT[:, h * 128 + 64:h * 128 + 128], state_m[:, hc], False, True,
                   r=["q_decT", "state_m"], w=["pb5"], tp=(0, 64))
            state_update(1, state_b, "state_b")
            for h in range(4):
                hc = slice(h * 128, (h + 1) * 128)
                ACT(junk[:, hc], PB[5][:, hc], AF.Square, r=["pb5"], w=["junk", "oss"], accum=oss[:, h:h + 1])
            rstd_from_ss(rso, oss, 128, r=["oss"], w=["rso"])
            for h in range(4):
                hc = slice(h * 128, (h + 1) * 128)
                STT("dve", on[:, hc], PB[5][:, hc], rso[:, h:h + 1], hgw, ALU.mult, ALU.mult,
                    r=["pb5", "rso", "hgw"], w=["on"])
            TT("pool", og, on, gate, ALU.mult, r=["on", "gate"], w=["og"])
            for h in range(4):
                TR(pb6[:, h * 128:(h + 1) * 128], og[:, h * 128:(h + 1) * 128], r=["og"], w=["pb6"])
            osl = (i // OSTN) % 2
            ti = i % OSTN
            CP("act", OST[osl][:, :, ti * 128:(ti + 1) * 128],
               pb6[:, 0:512].rearrange("p (h t) -> p h t", h=4), r=["pb6"], w=[f"ost{osl}"])
            if ti == OSTN - 1:
                T0 = (i // OSTN) * OSB
                dq, tq0 = T0 // S4, T0 % S4
                DMA("sp", a2a_in[dq, 0:512, tq0:tq0 + OSB].rearrange("(h p) t -> p h t", p=128), OST[osl],
                    f"ost{osl}", r=[f"ost{osl}"], w=["a2a_in"])
            if i == 0:
                DBG("dbg_o", on, ["on"])

    def phase1c():
        R.barrier(skip=("precast",))
        C = Carver(NOWBC_END)
        WML = C.b(KT * 1344).rearrange("p (k n) -> p k n", k=KT)
        WUQ = C.b(6 * 768).rearrange("p (k n) -> p k n", k=6)
        WUKV = C.b(4 * 1024).rearrange("p (k n) -> p k n", k=4)
        XNT = [C.b(KT * 128) for _ in range(2)]
        qnw_bc = C.f(768)
        kvnw_bc = C.f(512)
        qhw_bc = C.f(192)
        khw_bc = C.f(192)
        cos_all = C.f(NT * 32)
        sin_all = C.f(NT * 32)
        ang = C.f(NT * 32)
        posf = C.f(NT)
        posT = C.f(128)
        posTi = C.f(128, I32)
        invf_bc = C.f(32)
        identF = C.f(128)
        cqn = C.b(768)
        ckvn = C.b(512)
        cqnT = C.b(768)
        ckvnT = C.b(512)
        junk = C.b(768)
        krope = C.f(64)
        qf = C.f(768).rearrange("p (h d) -> p h d", h=4)
        kf = C.f(768).rearrange("p (h d) -> p h d", h=4)
        qb = C.b(768).rearrange("p (h d) -> p h d", h=4)
        kb = C.b(768).rearrange("p (h d) -> p h d", h=4)
        vb = C.b(512)
        tmp = [C.f(128).rearrange("p (h d) -> p h d", h=4) for _ in range(4)]
        tmpk = [C.f(128).rearrange("p (h d) -> p h d", h=4) for _ in range(4)]
        QKn_s = C.b(1024)
        QTn_s, KTn_s = QKn_s[:, 0:512], QKn_s[:, 512:1024]
        QTr_s = C.b(512)
        KTr_s = C.b(512)
        st = C.f(32)
        kc = max(1, KT // 8)
        for g in range(KT // kc):
            DMA("pool", WML[:, g * kc:(g + 1) * kc, :],
                w_ml[g * kc * 128:(g + 1) * kc * 128, :].rearrange("(k p) n -> p k n", p=128), "wml", w=["wml"])
        DMA("pool", WUQ, w_uq.rearrange("(k p) n -> p k n", p=128), "wuq", w=["wuq"])
        DMA("pool", WUKV, w_ukv.rearrange("(k p) n -> p k n", p=128), "wukv", w=["wukv"])
        DMA("sp", qnw_bc, qnw[0:1, :].broadcast_to([128, 768]), "c1", w=["qnw"])
        DMA("sp", kvnw_bc, kvnw[0:1, :].broadcast_to([128, 512]), "c2", w=["kvnw"])
        DMA("sp", qhw_bc, qhw[0:1, :].broadcast_to([128, 192]), "c3", w=["qhw"])
        DMA("sp", khw_bc, khw[0:1, :].broadcast_to([128, 192]), "c4", w=["khw"])
        DMA("sp", invf_bc, invf[0:1, :].broadcast_to([128, 32]), "c5", w=["invf"])
        DMA("sp", posTi[0:NT, :], pos[0:1, :].rearrange("o (n p) -> (o n) p", p=128), "c6", w=["posTi"])

        def _idf(e):
            e.memset(identF, 0.0)
            return e.affine_select(out=identF, in_=identF, pattern=[[-1, 128]], compare_op=ALU.not_equal,
                                   fill=1.0, base=0, channel_multiplier=1)
        R.add("pool", _idf, w=["identF"])
        CP("dve", posT[0:NT, :], posTi[0:NT, :], r=["posTi"], w=["posT"])
        MM(PB[0][:, 0:NT], posT[0:NT, :], identF[0:NT, 0:NT], True, True, r=["posT", "identF"], w=["pb0"])
        CP("dve", posf, PB[0][:, 0:NT], r=["pb0"], w=["posf"])
        for n in range(NT):
            TS("dve", ang[:, n * 32:(n + 1) * 32], invf_bc, posf[:, n:n + 1], None, ALU.mult,
               r=["invf", "posf"], w=["ang"])
        angi = C.f(NT * 32, I32)
        angn = C.f(NT * 32)
        for (dst, shift, nm) in ((sin_all, 0.5, "sin_all"), (cos_all, 0.75, "cos_all")):
            TS("dve", dst, ang, 1.0 / (2.0 * math.pi), shift, ALU.mult, ALU.add, r=["ang"], w=[nm])
            CP("dve", angi, dst, r=[nm], w=["angi"])
            CP("dve", angn, angi, r=["angi"], w=["angn"])
            TT("dve", dst, dst, angn, ALU.subtract, r=[nm, "angn"], w=[nm])
            TS("dve", angn, dst, 0.0, None, ALU.is_lt, r=[nm], w=["angn"])
            TT("dve", dst, dst, angn, ALU.add, r=[nm, "angn"], w=[nm])
            ACT(dst, dst, AF.Sin, r=[nm, "consts"], w=[nm], scale=2.0 * math.pi, bias=npi_t)
        pb4 = PB[4][:].bitcast(BF16)
        pb5 = PB[5][:].bitcast(BF16)
        import os as _os
        stop = int(_os.environ.get("PH1C_STOP", "9"))
        for i in range(NT if stop > 0 else 0):
            sl = i % 2
            xk = f"mxnt{sl}"
            t0 = i * 128
            DMA("sp", XNT[sl], xnT_d[i], xk, r=[f"xnTd{i}"], w=[xk])
            for (bi, c0, cw) in ((0, 0, 512), (1, 512, 320), (2, 832, 512)):
                for k in range(KT):
                    MM(PB[bi][:, 0:cw], XNT[sl][:, k * 128:(k + 1) * 128], WML[:, k, c0:c0 + cw],
                       k == 0, k == KT - 1, r=[xk, "wml"], w=[f"pb{bi}"])
            ACT(junk[:, 0:512], PB[0][:], AF.Square, r=["pb0"], w=["junk", "st0"], accum=st[:, 0:1])
            ACT(junk[:, 0:256], PB[1][:, 0:256], AF.Square, r=["pb1"], w=["junk", "st1"], accum=st[:, 1:2])
            ACT(junk[:, 0:512], PB[2][:], AF.Square, r=["pb2"], w=["junk", "st3"], accum=st[:, 3:4])
            TT("dve", st[:, 2:3], st[:, 0:1], st[:, 1:2], ALU.add, r=["st0", "st1"], w=["st2"])
            rstd_from_ss(st[:, 4:5], st[:, 2:3], 768, r=["st2"], w=["rsq"])
            rstd_from_ss(st[:, 5:6], st[:, 3:4], 512, r=["st3"], w=["rskv"])
            STT("dve", cqn[:, 0:512], PB[0][:], st[:, 4:5], qnw_bc[:, 0:512], ALU.mult, ALU.mult,
                r=["pb0", "rsq", "qnw"], w=["cqn"])
            STT("dve", cqn[:, 512:768], PB[1][:, 0:256], st[:, 4:5], qnw_bc[:, 512:768], ALU.mult, ALU.mult,
                r=["pb1", "rsq", "qnw"], w=["cqn"])
            CP("dve", krope, PB[1][:, 256:320], r=["pb1", "st1"], w=["krope"])
            STT("dve", ckvn, PB[2][:], st[:, 5:6], kvnw_bc, ALU.mult, ALU.mult,
                r=["pb2", "rskv", "kvnw"], w=["ckvn"])
            for k in range(6):
                TR(pb4[:, k * 128:(k + 1) * 128], cqn[:, k * 128:(k + 1) * 128], r=["cqn"], w=["pb4"])
            for k in range(4):
                TR(pb5[:, k * 128:(k + 1) * 128], ckvn[:, k * 128:(k + 1) * 128], r=["ckvn"], w=["pb5"])
            CP("act", cqnT, pb4[:, 0:768], r=["pb4"], w=["cqnT"])
            CP("dve", ckvnT, pb5[:, 0:512], r=["pb5"], w=["ckvnT"])
            if stop < 2:
                continue
            for half in range(2):
                for k in range(6):
                    MM(PB[6 + half][:, 0:384], cqnT[:, k * 128:(k + 1) * 128], WUQ[:, k, half * 384:(half + 1) * 384],
                       k == 0, k == 5, r=["cqnT", "wuq"], w=[f"pb{6 + half}"])
            for half in range(2):
                for k in range(4):
                    MM(PB[half][:], ckvnT[:, k * 128:(k + 1) * 128], WUKV[:, k, half * 512:(half + 1) * 512],
                       k == 0, k == 3, r=["ckvnT", "wukv"], w=[f"pb{half}"])
            CP("act", vb, PB[1][:], r=["pb1"], w=["vb"])
            DMA("sp", V_d[:, t0:t0 + 128, :].rearrange("h p d -> p h d"), vb.rearrange("p (h d) -> p h d", h=4),
                "vst", r=["vb"], w=["V_d"])
            for h in range(4):
                src = PB[6 + h // 2][:, (h % 2) * 192:(h % 2) * 192 + 192]
                ACT(junk[:, 0:192], src, AF.Square, r=[f"pb{6 + h // 2}"], w=["junk", "ssqh"], accum=st[:, 8 + h:9 + h])
            for h in range(4):
                ACT(junk[:, 0:128], PB[0][:, h * 128:(h + 1) * 128], AF.Square, r=["pb0"], w=["junk", "sskh"],
                    accum=st[:, 12 + h:13 + h])
            ACT(junk[:, 0:64], krope, AF.Square, r=["krope"], w=["junk", "sskr"], accum=st[:, 6:7])
            TS("dve", st[:, 12:16], st[:, 12:16], st[:, 6:7], None, ALU.add, r=["sskh", "sskr"], w=["sskh"])
            rstd_from_ss(st[:, 16:20], st[:, 8:12], 192, r=["ssqh"], w=["rsqh"])
            rstd_from_ss(st[:, 20:24], st[:, 12:16], 192, r=["sskh"], w=["rskh"])
            for h in range(4):
                src = PB[6 + h // 2][:, (h % 2) * 192:(h % 2) * 192 + 192]
                STT("dve", qf[:, h, :], src, st[:, 16 + h:17 + h], qhw_bc, ALU.mult, ALU.mult,
                    r=[f"pb{6 + h // 2}", "rsqh", "qhw"], w=["qf"])
                STT("dve", kf[:, h, 0:128], PB[0][:, h * 128:(h + 1) * 128], st[:, 20 + h:21 + h], khw_bc[:, 0:128],
                    ALU.mult, ALU.mult, r=["pb0", "rskh", "khw"], w=["kf"])
                STT("dve", kf[:, h, 128:192], krope, st[:, 20 + h:21 + h], khw_bc[:, 128:192],
                    ALU.mult, ALU.mult, r=["krope", "rskh", "khw"], w=["kf"])
            if stop < 3:
                continue
            cb = cos_all[:, i * 32:(i + 1) * 32].rearrange("p (o n) -> p o n", o=1).broadcast_to([128, 4, 32])
            sb_ = sin_all[:, i * 32:(i + 1) * 32].rearrange("p (o n) -> p o n", o=1).broadcast_to([128, 4, 32])
            for (src, dst, nm, eng) in ((qf, qb, "q", "pool"), (kf, kb, "k", "dve")):
                x1 = src[:, :, 128:160]
                x2 = src[:, :, 160:192]
                tk = [f"tmp{nm}{j}" for j in range(4)]
                tm = tmp if nm == "q" else tmpk
                TT(eng, tm[0], x1, cb, ALU.mult, r=[nm + "f", "cos_all"], w=[tk[0]])
                TT(eng, tm[1], x2, sb_, ALU.mult, r=[nm + "f", "sin_all"], w=[tk[1]])
                TT(eng, tm[2], x2, cb, ALU.mult, r=[nm + "f", "cos_all"], w=[tk[2]])
                TT(eng, tm[3], x1, sb_, ALU.mult, r=[nm + "f", "sin_all"], w=[tk[3]])
                TT(eng, dst[:, :, 128:160], tm[0], tm[1], ALU.subtract, r=[tk[0], tk[1]], w=[nm + "b"])
                TT(eng, dst[:, :, 160:192], tm[2], tm[3], ALU.add, r=[tk[2], tk[3]], w=[nm + "b"])
                CP(eng, dst[:, :, 0:128], src[:, :, 0:128], r=[nm + "f"], w=[nm + "b"])
            if stop < 4:
                continue
            var = _os.environ.get("PH1C_VAR", "")
            for h in range(4):
                if var == "notr":
                    continue
                TR(pb4[:, h * 128:(h + 1) * 128], qb[:, h, 0:128], r=["qb"], w=["pb4"])
                TR(pb4[:, 512 + h * 128:512 + (h + 1) * 128], kb[:, h, 0:128], r=["kb"], w=["pb4"])
            for h in range(4):
                if var == "nomm":
                    continue
                MM(PB[5][0:64, h * 128:(h + 1) * 128], qb[:, h, 128:192], IDB, True, True, r=["qb", "consts"], w=["pb5"])
                MM(PB[3][0:64, h * 128:(h + 1) * 128], kb[:, h, 128:192], IDB, True, True, r=["kb", "consts"], w=["pb3"])
            CP("act", QKn_s, pb4[:, 0:1024], r=["pb4"], w=["QTn_s", "KTn_s"])
            CP("dve", QTr_s[0:64, :], PB[5][0:64, :], r=["pb5"], w=["QTr_s"])
            CP("dve", KTr_s[0:64, :], PB[3][0:64, :], r=["pb3"], w=["KTr_s"])
            if stop < 5:
                continue
            for (dst_d, src_s, nm, npart) in ((QTn_d, QTn_s, "QTn", 128), (KTn_d, KTn_s, "KTn", 128),
                                             (QTr_d, QTr_s, "QTr", 64), (KTr_d, KTr_s, "KTr", 64)):
                DMA("sp", dst_d[:, :, t0:t0 + 128].rearrange("h p t -> p h t"),
                    src_s[0:npart, :].rearrange("p (h t) -> p h t", h=4), nm + "st", r=[nm + "_s"], w=[nm + "_d"])
            if i == 0:
                DBG("dbg_q", qf.rearrange("p h d -> p (h d)"), ["qf"])
                DBG("dbg_k", kf.rearrange("p h d -> p (h d)"), ["kf"])

    def phase1d():
        QB = min(512, S)
        NQB = S // QB
        KPB = QB // 128
        scale = 1.0 / math.sqrt(192.0)
        for h in range(4):
            R.barrier(skip=("precast",))
            C = Carver(NOWBC_END)
            KTn = C.b(S)
            KTr = C.b(S)
            Vh = C.b(NT * 128).rearrange("p (n d) -> p n d", n=NT)
            Qn = [C.b(QB) for _ in range(2)]
            Qr = [C.b(QB) for _ in range(2)]
            PT = [C.b(QB) for _ in range(3)]
            OT = [C.b(QB) for _ in range(2)]
            rec = C.f(QB)
            MSK = [C.b(QB) for _ in range(KPB)]
            u = f"a{h}"
            for d_ in range(KPB):
                def _mk(e, t=MSK[d_], d_=d_):
                    e.memset(t, 1.0)
                    return e.affine_select(out=t, in_=t, pattern=[[1, QB]], compare_op=ALU.is_ge, fill=0.0,
                                           base=-128 * d_, channel_multiplier=-1)
                R.add("pool", _mk, w=[u + f"msk{d_}"])
            DMA("sp", KTn, KTn_d[h], "ktn", r=["KTn_d"], w=[u + "ktn"])
            DMA("sp", KTr[0:64, :], KTr_d[h], "ktr", r=["KTr_d"], w=[u + "ktr"])
            DMA("sp", Vh, V_d[h].rearrange("(n p) d -> p n d", p=128), "vh", r=["V_d"], w=[u + "vh"])
            if h == 0 and do2 and not _os2.environ.get("SKIP_PRECAST"):
                precast("A")
            its = [(qb_, kt) for qb_ in range(NQB) for kt in range((qb_ + 1) * KPB)]
            LA = 2

            def emit_S(idx):
                qb_, kt = its[idx]
                qs_ = qb_ % 2
                q0 = qb_ * QB
                if kt == 0:
                    DMA("sp", Qn[qs_], QTn_d[h, :, q0:q0 + QB], f"qn{qs_}", r=["QTn_d"], w=[f"qn{qs_}"])
                    DMA("sp", Qr[qs_][0:64, :], QTr_d[h, :, q0:q0 + QB], f"qr{qs_}", r=["QTr_d"], w=[f"qr{qs_}"])
                bi = idx % 4
                MM(PB[bi][:, 0:QB], KTn[:, kt * 128:(kt + 1) * 128], Qn[qs_], True, False,
                   r=[u + "ktn", f"qn{qs_}"], w=[f"pb{bi}"])
                MM(PB[bi][:, 0:QB], KTr[0:64, kt * 128:(kt + 1) * 128], Qr[qs_][0:64, :], False, True,
                   r=[u + "ktr", f"qr{qs_}"], w=[f"pb{bi}"])

            def emit_rest(idx):
                qb_, kt = its[idx]
                qs_ = qb_ % 2
                q0 = qb_ * QB
                nkt = (qb_ + 1) * KPB
                ob, lbk = 4 + qs_, 6 + qs_
                bi = idx % 4
                ps_ = idx % 3
                ACT(PT[ps_], PB[bi][:, 0:QB], AF.Exp, r=[f"pb{bi}"], w=[f"pt{ps_}"], scale=scale)
                d = kt - qb_ * KPB
                if d >= 0:
                    TT("dve", PT[ps_], PT[ps_], MSK[d], ALU.mult, r=[f"pt{ps_}", u + f"msk{d}"], w=[f"pt{ps_}"])
                MM(PB[ob][:, 0:QB], Vh[:, kt, :], PT[ps_], kt == 0, kt == nkt - 1,
                   r=[u + "vh", f"pt{ps_}"], w=[f"pb{ob}"])
                MM(PB[lbk][:, 0:QB], ones_b, PT[ps_], kt == 0, kt == nkt - 1,
                   r=["consts", f"pt{ps_}"], w=[f"pb{lbk}"])
                if kt == nkt - 1:
                    R.add("dve", lambda e, o=rec, i_=PB[lbk][:, 0:QB]: e.reciprocal(out=o, in_=i_),
                          r=[f"pb{lbk}"], w=["rec"])
                    TT("dve", OT[qs_], PB[ob][:, 0:QB], rec, ALU.mult, r=[f"pb{ob}", "rec"], w=[f"ot{qs_}"])
                    OSB = min(QB, S4)
                    for part in range(QB // OSB):
                        T0 = q0 + part * OSB
                        dq, tq0 = T0 // S4, T0 % S4
                        DMA("sp", a2a_in[dq, 512 + h * 128:512 + (h + 1) * 128, tq0:tq0 + OSB],
                            OT[qs_][:, part * OSB:(part + 1) * OSB], f"ot{qs_}", r=[f"ot{qs_}"], w=["a2a_in"])

            for idx in range(len(its) + LA):
                if idx < len(its):
                    emit_S(idx)
                if idx - LA >= 0:
                    emit_rest(idx - LA)

    if do1:
        x_seq = din("x_seq", [S, D])
        pos = din("pos", [1, S], I32)
        w_hg = din("w_hg", [D, 2048])
        w_ml = din("w_ml", [D, 1344])
        n1w = din("n1w", [1, D])
        lb_raw = din("lb_raw", [2, 512])
        hgnw = din("hgnw", [1, 128])
        qnw = din("qnw", [1, 768])
        kvnw = din("kvnw", [1, 512])
        qhw = din("qhw", [1, 192])
        khw = din("khw", [1, 192])
        w_uq = din("w_uq", [768, 768])
        w_ukv = din("w_ukv", [512, 1024])
        invf = din("invf", [1, 32])
        xnT_d = dscr("xnT_d", [NT, 128, KT * 128], BF16)
        QTn_d = dscr("QTn_d", [4, 128, S], BF16)
        QTr_d = dscr("QTr_d", [4, 64, S], BF16)
        KTn_d = dscr("KTn_d", [4, 128, S], BF16)
        KTr_d = dscr("KTr_d", [4, 64, S], BF16)
        V_d = dscr("V_d", [4, S, 128], BF16)
    if mode == "p1":
        a2a_in = dout("a2a_in", [4, 1024, S4], BF16)
    elif mode == "fused":
        a2a_in = dscr("a2a_in", [4, 1024, S4], BF16)
    if mode == "p2":
        mix_all = din("mix_all", [4, 1024, S4], BF16)
    elif mode == "fused":
        CR = min(1024, (1 << 20) // (S4 * 2))
        NQ = 1024 // CR
        gathered = dscr("gathered", [4, NQ, 4, CR, S4], BF16)
        sel = din("sel", [1, 4])
    if do2:
        x_res = din("x_res", [S4, D])
        n2w = din("n2w", [1, D])
        w_o = din("w_o", [4096, D])
        w_up = din("w_up", [D, DFF])
        w_dn = din("w_dn", [DFF, D])
        out = dout("out", [S4, D], F32)
        hbuf = dscr("hbuf", [S4, D], F32)
        CW = min(512, D)
        NCW = D // CW
        SUB = min(16, FH)
        wo_t = dscr("wo_t", [NCW, 128, 32 * CW], BF16)
        wup_b = dscr("wup_b", [D, DFF], BF16)
        wupv = wup_b.rearrange("(k p) n -> p k n", p=128)
        wdn_t = dscr("wdn_t", [NCW, DFF // (128 * SUB), 128, SUB * CW], BF16)

    if do2 and not do1 and not _os2.environ.get("SKIP_PRECAST"):
        precast("A")
        precast("B")
    if do1:
        import os as _os
        upto = _os.environ.get("PH1_UPTO", "d")
        phase1a()
        if upto >= "b":
            phase1b()
        if upto >= "c":
            phase1c()
        if upto >= "d":
            phase1d()
    if mode == "fused":
        R.barrier(skip=("precast",))
        def AG(in_ap, out_ap):
            R.add("pool", lambda e: e.collective_compute(
                "AllGather", ALU.bypass, replica_groups=[[0, 1, 2, 3], [4, 5, 6, 7]],
                ins=[in_ap], outs=[out_ap]), r=["a2a_in"], w=["mix_all"], dma="ag", inc=1)
        for d in range(4 if not _os2.environ.get("SKIP_AG") else 0):
            for q in range(NQ):
                AG(a2a_in[d, q * CR:(q + 1) * CR, :].opt(),
                   gathered[d, q].rearrange("s r t -> (s r) t").opt())
    if do2 and do1 and not _os2.environ.get("SKIP_PRECAST"):
        precast("B")
    if do2:
        phase2()

    R.barrier()
    R.add("sp", lambda e: e.nop(), r=[], w=[])

    sems = {}
    for en in Rec.ENGS:
        s = nc.semaphore("s_" + en)
        ctx.append(s)
        sems[en] = s.__enter__()
    dsems = {}
    for i, k in enumerate(R.dma_keys()):
        s = nc.semaphore(f"d{i}")
        ctx.append(s)
        dsems[k] = s.__enter__()
    with nc.Block() as block:
        R.emit(block, sems, dsems)
    return nc


MODE = "fused"
_NC_CACHE = {}


def _get_nc(cfg, mode):
    key = (cfg.D, cfg.S, cfg.DFF, mode)
    if key not in _NC_CACHE:
        _NC_CACHE[key] = build(cfg, mode)
    return _NC_CACHE[key]


def shard_inputs(inp):
    x = np.asarray(inp["x"])
    B, S, D = x.shape
    w_in = np.asarray(inp["w_in"])[0]
    splits = [2048, 2048, 2048, 2048, 768, 512, 64]
    offs = np.cumsum([0] + splits)
    q_, f_, i_, g_, cq_, ckv_, kr_ = [w_in[:, offs[k]:offs[k + 1]] for k in range(7)]
    w_ml = np.ascontiguousarray(np.concatenate([cq_, kr_, ckv_], axis=1))
    w_o = np.asarray(inp["w_o"])[0]
    w_o_p = np.ascontiguousarray(np.concatenate(
        [np.concatenate([w_o[s * 512:(s + 1) * 512], w_o[2048 + s * 512:2048 + (s + 1) * 512]], axis=0)
         for s in range(4)], axis=0))
    w_up = np.ascontiguousarray(np.asarray(inp["w_up"])[0])
    w_dn = np.ascontiguousarray(np.asarray(inp["w_down"])[0])
    invf = (np.float32(10000.0) ** (-np.arange(0, 64, 2, dtype=np.float32) / np.float32(64))).astype(np.float32)[None, :]
    pos = np.asarray(inp["positions"]).astype(np.int32)
    S4 = S // 4
    maps1, maps2 = [], []
    for c in range(8):
        b, j = c // 4, c % 4
        hs = slice(j * 512, (j + 1) * 512)
        wukv = np.asarray(inp["w_ukv"])[0].reshape(512, 16, 256)[:, 4 * j:4 * j + 4, :]
        m1 = {
            "x_seq": np.ascontiguousarray(x[b]),
            "pos": np.ascontiguousarray(pos[b:b + 1]),
            "w_hg": np.ascontiguousarray(np.concatenate([q_[:, hs], f_[:, hs], i_[:, hs], g_[:, hs]], axis=1)),
            "w_ml": w_ml,
            "n1w": np.ascontiguousarray(np.asarray(inp["norm1_w"])[0:1]),
            "lb_raw": np.ascontiguousarray(np.asarray(inp["hgrn_lb"])[:, hs]),
            "hgnw": np.ascontiguousarray(np.asarray(inp["hgrn_out_norm_w"])[0:1]),
            "qnw": np.ascontiguousarray(np.asarray(inp["mla_q_norm_w"])[0:1]),
            "kvnw": np.ascontiguousarray(np.asarray(inp["mla_kv_norm_w"])[0:1]),
            "qhw": np.ascontiguousarray(np.asarray(inp["q_head_norm_w"])[0:1]),
            "khw": np.ascontiguousarray(np.asarray(inp["k_head_norm_w"])[0:1]),
            "w_uq": np.ascontiguousarray(np.asarray(inp["w_uq"])[0][:, j * 768:(j + 1) * 768]),
            "w_ukv": np.ascontiguousarray(np.concatenate(
                [wukv[:, :, :128].reshape(512, 512), wukv[:, :, 128:].reshape(512, 512)], axis=1)),
            "invf": invf,
        }
        selv = np.zeros((1, 4), np.float32)
        selv[0, j] = 1.0
        m2 = {
            "sel": selv,
            "x_res": np.ascontiguousarray(x[b, j * S4:(j + 1) * S4]),
            "n2w": np.ascontiguousarray(np.asarray(inp["norm2_w"])[0:1]),
            "w_o": w_o_p,
            "w_up": w_up,
            "w_dn": w_dn,
        }
        maps1.append(m1)
        maps2.append(m2)
    return maps1, maps2


def kernel(**inputs):
    x = np.asarray(inputs["x"])
    B, S, D = x.shape
    DFF = np.asarray(inputs["w_up"]).shape[-1]
    cfg = Cfg(D=D, S=S, DFF=DFF)
    S4 = S // 4
    maps1, maps2 = shard_inputs(inputs)
    cores = list(range(8))
    if MODE == "fused":
        nc = _get_nc(cfg, "fused")
        res = run_bass_kernel_spmd(nc, [dict(a, **b_) for a, b_ in zip(maps1, maps2)], core_ids=cores)
        outs = [r["out"] for r in res.results]
    else:
        nc1 = _get_nc(cfg, "p1")
        r1 = run_bass_kernel_spmd(nc1, maps1, core_ids=cores).results
        for c in range(8):
            b, j = c // 4, c % 4
            maps2[c].pop("sel", None)
            maps2[c]["mix_all"] = np.ascontiguousarray(
                np.stack([np.asarray(r1[b * 4 + s]["a2a_in"])[j] for s in range(4)], axis=0))
        nc2 = _get_nc(cfg, "p2")
        r2 = run_bass_kernel_spmd(nc2, maps2, core_ids=cores).results
        outs = [r["out"] for r in r2]
    full = np.empty((B, S, D), np.float32)
    for c in range(8):
        b, j = c // 4, c % 4
        full[b, j * S4:(j + 1) * S4] = np.asarray(outs[c], dtype=np.float32)
    return full
```
